# Optimizing a Trainium2 kernel written in Bass

```python
import math
import jax, jax.numpy as jnp
from jax import lax
import numpy as np

D_MODEL = 1024
BATCH = 4
SEQ = 4096
DEPTH = 4

CHUNK = 64
Q_BLOCK = 128
N_HEADS = 8
HEAD_DIM = 64
V_DIM = 2 * HEAD_DIM
QK_WIDTH = N_HEADS * 2 * HEAD_DIM
ATTN_WIDTH = N_HEADS * V_DIM
ROT_DIM = HEAD_DIM // 4
ROPE_THETA = 500000.0
CONV_WIDTH = D_MODEL
CONV_K = 3
D_FF = -(-8 * D_MODEL // (3 * 256)) * 256
IN_WIDTH = 3 * QK_WIDTH + 3 * CONV_WIDTH + 2 * D_MODEL
SPLITS = tuple(int(v) for v in np.cumsum([QK_WIDTH, QK_WIDTH, ATTN_WIDTH, CONV_WIDTH, CONV_WIDTH, CONV_WIDTH, D_MODEL]))
NORM_EPS = 1e-6
NEG_INF = -1e30

kernel_name = "chunk_causal_hybrid_diffattn_shortconv"


def rms_norm(x, w):
    xf = x.astype(jnp.float32)
    y = xf * lax.rsqrt(jnp.mean(xf * xf, axis=-1, keepdims=True) + NORM_EPS)
    return (y * w.astype(jnp.float32)).astype(x.dtype)


def rope_tables(positions):
    freqs = ROPE_THETA ** (-jnp.arange(0, ROT_DIM, 2, dtype=jnp.float32) / ROT_DIM)
    ang = positions.astype(jnp.float32)[..., None] * freqs
    return jnp.cos(ang), jnp.sin(ang)


def apply_partial_rope(t, cos, sin):
    c = cos[:, :, None, None, :]
    s = sin[:, :, None, None, :]
    rot = t[..., :ROT_DIM].astype(jnp.float32)
    x1, x2 = rot[..., : ROT_DIM // 2], rot[..., ROT_DIM // 2:]
    r = jnp.concatenate([x1 * c - x2 * s, x2 * c + x1 * s], axis=-1)
    return jnp.concatenate([r.astype(t.dtype), t[..., ROT_DIM:]], axis=-1)


def diff_attention(q, k, v, chunk_id, lam):
    b, s = q.shape[0], q.shape[1]
    nb = s // Q_BLOCK
    scale = HEAD_DIM ** -0.5
    kt = k.transpose(0, 2, 3, 1, 4)
    vt = v.transpose(0, 2, 1, 3)
    qb = q.reshape(b, nb, Q_BLOCK, N_HEADS, 2, HEAD_DIM).transpose(1, 0, 3, 4, 2, 5)
    qcb = chunk_id.reshape(b, nb, Q_BLOCK).transpose(1, 0, 2)

    def block(args):
        qi, qci = args
        sc = jnp.einsum('bhmqd,bhmkd->bhmqk', qi, kt).astype(jnp.float32) * scale
        allowed = chunk_id[:, None, :] <= qci[:, :, None]
        sc = jnp.where(allowed[:, None, None], sc, NEG_INF)
        p = jax.nn.softmax(sc, axis=-1)
        a = p[:, :, 0] - lam * p[:, :, 1]
        return jnp.einsum('bhqk,bhkd->bhqd', a.astype(vt.dtype), vt)

    o = lax.map(block, (qb, qcb))
    return o.transpose(1, 0, 3, 2, 4).reshape(b, s, N_HEADS, V_DIM)


def causal_dwconv(u, w):
    return lax.conv_general_dilated(
        u, w[:, None, :].astype(u.dtype), window_strides=(1,),
        padding=[(CONV_K - 1, 0)], dimension_numbers=('NWC', 'WIO', 'NWC'),
        feature_group_count=u.shape[-1])


def setup_inputs(seed: int = 0) -> dict:
    key = jax.random.key(seed)
    ks = jax.random.split(key, 20)
    f32 = jnp.float32

    def nrm(k, shape, fan_in):
        return jax.random.normal(k, shape, f32) * (fan_in ** -0.5)

    x = jax.random.normal(ks[0], (BATCH, SEQ, D_MODEL), f32)
    start = jax.random.randint(ks[1], (BATCH, 1), 0, 64, dtype=jnp.int32) * CHUNK
    positions = start + jnp.arange(SEQ, dtype=jnp.int32)[None, :]
    return {
        "x": x,
        "positions": positions,
        "mix_norm": 1.0 + 0.01 * jax.random.normal(ks[2], (DEPTH, D_MODEL), f32),
        "w_in": nrm(ks[3], (DEPTH, D_MODEL, IN_WIDTH), D_MODEL),
        "lambda_q1": 0.1 * jax.random.normal(ks[4], (DEPTH, HEAD_DIM), f32),
        "lambda_k1": 0.1 * jax.random.normal(ks[5], (DEPTH, HEAD_DIM), f32),
        "lambda_q2": 0.1 * jax.random.normal(ks[6], (DEPTH, HEAD_DIM), f32),
        "lambda_k2": 0.1 * jax.random.normal(ks[7], (DEPTH, HEAD_DIM), f32),
        "subln_w": 1.0 + 0.01 * jax.random.normal(ks[8], (DEPTH, V_DIM), f32),
        "conv_w": nrm(ks[9], (DEPTH, CONV_K, CONV_WIDTH), CONV_K),
        "w_branch_a": nrm(ks[10], (DEPTH, ATTN_WIDTH, D_MODEL), ATTN_WIDTH),
        "w_branch_b": nrm(ks[11], (DEPTH, CONV_WIDTH, D_MODEL), CONV_WIDTH),
        "w_out": nrm(ks[12], (DEPTH, D_MODEL, D_MODEL), D_MODEL),
        "ffn_norm": 1.0 + 0.01 * jax.random.normal(ks[13], (DEPTH, D_MODEL), f32),
        "w_gate": nrm(ks[14], (DEPTH, D_MODEL, D_FF), D_MODEL),
        "w_up": nrm(ks[15], (DEPTH, D_MODEL, D_FF), D_MODEL),
        "w_down": nrm(ks[16], (DEPTH, D_FF, D_MODEL), D_FF),
        "final_norm": 1.0 + 0.01 * jax.random.normal(ks[17], (D_MODEL,), f32),
    }


def reference(x, positions, mix_norm, w_in, lambda_q1, lambda_k1, lambda_q2, lambda_k2,
              subln_w, conv_w, w_branch_a, w_branch_b, w_out, ffn_norm,
              w_gate, w_up, w_down, final_norm):
    b, s, _ = x.shape
    cos, sin = rope_tables(positions)
    chunk_id = positions // CHUNK
    for l in range(DEPTH):
        lambda_init = 0.8 - 0.6 * math.exp(-0.3 * l)
        xn = rms_norm(x, mix_norm[l])
        proj = xn @ w_in[l]
        q, k, v, bg, cg, u, ga, gb = jnp.split(proj, SPLITS, axis=-1)
        q = apply_partial_rope(q.reshape(b, s, N_HEADS, 2, HEAD_DIM), cos, sin)
        k = apply_partial_rope(k.reshape(b, s, N_HEADS, 2, HEAD_DIM), cos, sin)
        v = v.reshape(b, s, N_HEADS, V_DIM)
        lam = (jnp.exp(jnp.sum(lambda_q1[l].astype(jnp.float32) * lambda_k1[l].astype(jnp.float32)))
               - jnp.exp(jnp.sum(lambda_q2[l].astype(jnp.float32) * lambda_k2[l].astype(jnp.float32)))
               + lambda_init)
        o = diff_attention(q, k, v, chunk_id, lam)
        o = rms_norm(o, subln_w[l]) * (1.0 - lambda_init)
        y_a = o.reshape(b, s, ATTN_WIDTH) @ w_branch_a[l]
        y_b = (bg * causal_dwconv(cg * u, conv_w[l])) @ w_branch_b[l]
        mixed = jax.nn.sigmoid(ga) * y_a + jax.nn.sigmoid(gb) * y_b
        x = x + mixed @ w_out[l]
        hn = rms_norm(x, ffn_norm[l])
        x = x + (jax.nn.silu(hn @ w_gate[l]) * (hn @ w_up[l])) @ w_down[l]
    return rms_norm(x, final_norm)
```

```python
import math
import os
from contextlib import ExitStack

import numpy as np
import concourse.bass as bass
import concourse.mybir as mybir
from concourse.bass_utils import run_bass_kernel_spmd

F32 = mybir.dt.float32
BF16 = mybir.dt.bfloat16
I32 = mybir.dt.int32
AF = mybir.ActivationFunctionType
ALU = mybir.AluOpType
AX = mybir.AxisListType

D = 1024
DEPTH = 4
NH = 8
DFF = 2816
NFC = DFF // 128
TOK = 2048
NTB = 4
NTT = 16
EPS = 1e-6
THETA = 500000.0
GROUPS = [[0, 1], [2, 3], [4, 5], [6, 7]]
MAGIC = 12582912.0
NEG = -30000.0

SEM_LIMIT = 2000
DSEM_LIMIT = 2000


class Buf:
    __slots__ = ("name", "last_write", "reads", "dsem", "excl")

    def __init__(self, name, excl=False):
        self.name = name
        self.excl = excl
        self.last_write = None
        self.reads = {}
        self.dsem = None


class DSem:
    __slots__ = ("handle", "count")

    def __init__(self, handle):
        self.handle = handle
        self.count = 0


class Op:
    __slots__ = ("eng", "fn", "deps", "signal", "tok", "is_dma", "dsem", "inc_amt", "seq")
    _ctr = [0]

    def __init__(self, eng, fn):
        Op._ctr[0] += 1
        self.seq = Op._ctr[0]
        self.eng = eng
        self.fn = fn
        self.deps = []
        self.signal = False
        self.tok = None
        self.is_dma = False
        self.dsem = None
        self.inc_amt = 1


def rkey(op):
    return ("d", id(op.dsem)) if op.is_dma else op.eng


class Prog:
    ENGINES = ("pe", "act", "dve", "pool", "sp")

    def __init__(self, nc, same_engine_sync=True):
        self.nc = nc
        self.ops = {e: [] for e in self.ENGINES}
        self.same_engine_sync = same_engine_sync
        self.nsem = 0
        self.stack = ExitStack()

    def new_sem(self, name):
        self.nsem += 1
        return self.stack.enter_context(self.nc.semaphore(f"{name}_{self.nsem}"))

    def sbuf(self, name, shape, dtype):
        return self.stack.enter_context(self.nc.sbuf_tensor(name, list(shape), dtype))

    def psum(self, name, shape, dtype):
        return self.stack.enter_context(self.nc.psum_tensor(name, list(shape), dtype))

    def _dep(self, op, prod):
        if prod is None or prod is op:
            return
        if prod.eng == op.eng and not prod.is_dma:
            if op.eng == "pe":
                return
            if not self.same_engine_sync:
                return
        if prod.is_dma:
            op.deps.append((prod, prod.dsem.count))
        else:
            prod.signal = True
            op.deps.append((prod, None))

    def _record(self, op, reads, writes):
        for b in reads:
            self._dep(op, b.last_write)
            if b.excl:
                for r in b.reads.values():
                    if r.eng != op.eng:
                        self._dep(op, r)
        for b in writes:
            self._dep(op, b.last_write)
            for r in b.reads.values():
                self._dep(op, r)
        for b in reads:
            b.reads[rkey(op)] = op
        for b in writes:
            b.last_write = op
            b.reads = {}
        self.ops[op.eng].append(op)
        return op

    def op(self, eng, fn, reads=(), writes=()):
        return self._record(Op(eng, fn), reads, writes)

    def dma(self, eng, fn, reads=(), writes=(), sem_buf=None, inc_amt=16):
        o = Op(eng, fn)
        o.is_dma = True
        o.inc_amt = inc_amt
        sb = sem_buf if sem_buf is not None else writes[0]
        if sb.dsem is None or sb.dsem.count + inc_amt > DSEM_LIMIT:
            sb.dsem = DSem(self.new_sem("d"))
        o.dsem = sb.dsem
        self._record(o, reads, writes)
        o.dsem.count += inc_amt
        return o

    def barrier(self, eng, deps):
        o = Op(eng, None)
        for p in deps:
            self._dep(o, p)
        self.ops[eng].append(o)
        return o

    def emit(self):
        nc = self.nc
        for e in self.ENGINES:
            sem = None
            cnt = 0
            for o in self.ops[e]:
                if o.is_dma or not o.signal:
                    continue
                if sem is None or cnt >= SEM_LIMIT:
                    sem = self.new_sem("e")
                    cnt = 0
                cnt += 1
                o.tok = (sem, cnt)
        engmap = {"pe": "tensor", "act": "scalar", "dve": "vector", "pool": "gpsimd", "sp": "sync"}
        with nc.Block() as block:
            for e in self.ENGINES:
                ops = self.ops[e]
                if not ops:
                    continue

                def body(engine, ops=ops):
                    waited = {}
                    for o in ops:
                        for (p, dval) in o.deps:
                            if p.is_dma:
                                s, v = p.dsem.handle, dval
                            else:
                                s, v = p.tok
                            k = s.num
                            if waited.get(k, 0) >= v:
                                continue
                            waited[k] = v
                            engine.wait_ge(s, v)
                        if o.fn is None:
                            continue
                        ins = o.fn(engine)
                        if o.is_dma:
                            ins.then_inc(o.dsem.handle, o.inc_amt)
                        elif o.signal:
                            ins.then_inc(o.tok[0], 1)

                getattr(block, engmap[e])(body)
        self.stack.close()


def retire(bufs):
    ops = {}
    for b in bufs:
        for o in ([b.last_write] if b.last_write is not None else []) + list(b.reads.values()):
            k = rkey(o)
            if k not in ops or ops[k].seq < o.seq:
                ops[k] = o
    return list(ops.values())


def inherit(bufs, ops):
    for b in bufs:
        for o in ops:
            k = rkey(o)
            if k not in b.reads or b.reads[k].seq < o.seq:
                b.reads[k] = o


def build_program(depth=DEPTH, stop=None):
    nc = bass.Bass("TRN2", target_bir_lowering=False)
    P = Prog(nc)

    def din(name, shape, dt):
        return nc.dram_tensor(name, list(shape), dt, kind="ExternalInput").ap()

    x_d = din("x", [TOK, D], F32)
    pos_d = din("pos", [128, NTT], I32)
    maskb_d = din("maskb", [128, NTB], F32)
    hmask_d = din("hmask", [128, NTB], F32)
    mixn_d = din("mix_norm", [DEPTH, D], F32)
    win_d = din("w_in", [depth, D, 8 * D], F32)
    lq1_d = din("lambda_q1", [DEPTH, 64], F32)
    lk1_d = din("lambda_k1", [DEPTH, 64], F32)
    lq2_d = din("lambda_q2", [DEPTH, 64], F32)
    lk2_d = din("lambda_k2", [DEPTH, 64], F32)
    subw_d = din("subln_w", [DEPTH, 128], F32)
    convw_d = din("conv_w", [DEPTH, 3, D], F32)
    wa_d = din("w_branch_a", [depth, D, D], F32)
    wb_d = din("w_branch_b", [depth, D, D], F32)
    wo_d = din("w_out", [depth, D, D], F32)
    ffnn_d = din("ffn_norm", [DEPTH, D], F32)
    wg_d = din("w_gate", [depth, D, DFF], F32)
    wu_d = din("w_up", [depth, D, DFF], F32)
    wd_d = din("w_down", [depth, DFF, D], F32)
    fin_d = din("final_norm", [D], F32)
    out_d = nc.dram_tensor("out", [TOK, D], F32, kind="ExternalOutput").ap()

    kvm = [nc.dram_tensor(f"kvm{i}", [128, 4096], BF16, kind="Internal").ap() for i in range(2)]
    kvg = [nc.dram_tensor(f"kvg{i}", [256, 4096], BF16, kind="Internal").ap() for i in range(2)]
    halm = nc.dram_tensor("halm", [512, 16], BF16, kind="Internal").ap()
    halg = nc.dram_tensor("halg", [1024, 16], BF16, kind="Internal").ap()
    on_dram = nc.dram_tensor("on_dram", [D, TOK], BF16, kind="Internal").ap()
    gb_dram = nc.dram_tensor("gb_dram", [D, TOK], BF16, kind="Internal").ap()
    Bkvm = [Buf("kvm0"), Buf("kvm1")]
    Bkvg = [Buf("kvg0"), Buf("kvg1")]
    Bhalm, Bhalg = Buf("halm"), Buf("halg")
    Bon_d = [Buf(f"on_d{h}") for h in range(NH)]
    Bgb_d = [Buf(f"gb_d{n}") for n in range(8)]

    xT = P.sbuf("xT", [128, 8, TOK], F32)
    BxT = [[Buf(f"xT{c}_{tb}") for tb in range(NTB)] for c in range(8)]
    xnT = P.sbuf("xnT", [128, 8, TOK], BF16)
    BxnT = [Buf(f"xnT{tb}") for tb in range(NTB)]
    BIG = P.sbuf("BIG", [128, 32768], BF16)
    ZO, KO = 0, 16384
    NSLOT = 3
    wslot = [P.sbuf(f"wslot{i}", [128, 3072], BF16) for i in range(NSLOT)]
    Bslot = [Buf(f"wslot{i}") for i in range(NSLOT)]
    NFT = 8
    ftall = P.sbuf("ftall", [128, NFT * 512], F32)
    ftmp = [ftall[:, i * 512:(i + 1) * 512] for i in range(NFT)]
    Bftmp = [Buf(f"ftmp{i}") for i in range(NFT)]
    ft_ctr = [0]

    def ft():
        i = ft_ctr[0] % NFT
        ft_ctr[0] += 1
        return ftmp[i], Bftmp[i]

    def ft2():
        if ft_ctr[0] % 2:
            ft_ctr[0] += 1
        i = ft_ctr[0] % NFT
        ft_ctr[0] += 2
        return ftall[:, i * 512:(i + 2) * 512].rearrange("p (m q) -> p m q", m=2), [Bftmp[i], Bftmp[i + 1]]

    ident_b = P.sbuf("ident_b", [128, 128], BF16)
    ident_f = P.sbuf("ident_f", [128, 128], F32)
    ones_b = P.sbuf("ones_b", [128, 128], BF16)
    ones_f = P.sbuf("ones_f", [128, 128], F32)
    Bconst = Buf("const")
    ropeC = P.sbuf("ropeC", [128, NTT, 16], F32)
    ropeS = P.sbuf("ropeS", [128, NTT, 16], F32)
    Brope = Buf("rope")
    nw = P.sbuf("nw", [128, 2 * DEPTH + 1, 8], F32)
    cw = P.sbuf("cw", [128, DEPTH, 3, 8], F32)
    neglam = P.sbuf("neglam", [128, DEPTH], F32)
    subs = P.sbuf("subs", [128, DEPTH], F32)
    maskb = P.sbuf("maskb_s", [128, NTB], F32)
    hmask = P.sbuf("hmask_s", [128, NTB], F32)
    Bparams = Buf("params")
    zh = P.sbuf("zh", [128, NTB, 16], BF16)
    Bzh = Buf("zh")
    rt1 = [P.sbuf(f"rt1_{i}", [128, 4, 16], F32) for i in range(2)]
    rt2 = [P.sbuf(f"rt2_{i}", [128, 4, 16], F32) for i in range(2)]
    Brt = [Buf(f"rt{i}") for i in range(2)]
    zero_b = P.sbuf("zero_b", [128, 16], BF16)

    psall = P.psum("psall", [128, 4096], F32)
    banks = [psall[:, i * 512:(i + 1) * 512] for i in range(8)]
    Bbank = [Buf(f"bank{i}", excl=True) for i in range(8)]
    bank_ctr = [0]
    bank_mod = [8]

    def nb():
        i = bank_ctr[0] % bank_mod[0]
        bank_ctr[0] += 1
        return banks[i], Bbank[i]

    def bigv(off, n):
        return BIG[:, off:off + n]

    zT = bigv(ZO, 16384).rearrange("p (c t) -> p c t", c=8)
    BzT = [[Buf(f"zT{c}_{tb}") for tb in range(NTB)] for c in range(8)]
    convtmp = bigv(KO, 4096).bitcast(F32)
    Bconv = Buf("convtmp")
    zx = bigv(KO + 4096, 4 * 514).rearrange("p (j t) -> p j t", j=4)
    Bzx = Buf("zx")
    mkv = [bigv(KO + 4096 * i, 4096) for i in range(2)]
    okv = [bigv(KO + 8192 + 4096 * i, 4096) for i in range(2)]
    Bmkv = [Buf("mkv0"), Buf("mkv1")]
    Bokv = [Buf("okv0"), Buf("okv1")]
    qT = [bigv(ZO + 2048 * i, 2048) for i in range(2)]
    BqT = [Buf("qT0"), Buf("qT1")]
    NPB = 8
    p1 = [bigv(ZO + 4096 + 1024 * i, 512) for i in range(NPB)]
    p2 = [bigv(ZO + 4096 + 1024 * i + 512, 512) for i in range(NPB)]
    pp = [bigv(ZO + 4096 + 1024 * i, 1024).rearrange("p (m q) -> p m q", m=2) for i in range(NPB)]
    osb = [bigv(ZO + 12288 + 2048 * i, 2048).bitcast(F32).rearrange("p (m q) -> p m q", m=2) for i in range(1)]
    Bosb = [Buf("osb0")]
    Bp1 = [Buf(f"p1_{i}") for i in range(NPB)]
    Bp2 = [Buf(f"p2_{i}") for i in range(NPB)]
    onst = [bigv(ZO + 14336 + 512 * i, 512) for i in range(2)]
    Bonst = [Buf("onst0"), Buf("onst1")]
    qkbf = [bigv(ZO + 15360 + 256 * i, 256) for i in range(2)]
    Bqkbf = [Buf("qkbf0"), Buf("qkbf1")]
    gbst = [bigv(ZO + 10240 + 512 * i, 512) for i in range(3)]
    mixT = bigv(KO, 16384).rearrange("p (c t) -> p c t", c=8)
    BmixT = [[Buf(f"mix{c}_{tb}") for tb in range(NTB)] for c in range(8)]
    hT = bigv(0, 11 * TOK).rearrange("p (c t) -> p c t", c=11)
    BhT = [[Buf(f"hT{c}_{tb}") for tb in range(NTB)] for c in range(11)]
    xstage = [bigv(2048 * i, 2048).bitcast(F32) for i in range(2)]
    Bxstage = [Buf("xst0"), Buf("xst1")]
    gbst4 = [bigv(KO + 8192 + 512 * i, 512) for i in range(3)]
    Bgbst4 = [Buf(f"gbst4_{i}") for i in range(3)]
    gbld = [P.sbuf(f"gbld{i}", [128, 512], BF16) for i in range(2)]
    Bgbld = [Buf("gbld0"), Buf("gbld1")]

    region = {"Z": [], "K": []}

    def switch(reg, new_bufs):
        ops = retire(region[reg])
        inherit(new_bufs, ops)
        region[reg] = list(new_bufs)

    def flat(ll):
        return [b for row in ll for b in row]

    wctr = [0]
    pending = []

    def kview(w2d):
        return w2d.rearrange("(kc kp) n -> kp kc n", kp=128)

    class Piece:
        def __init__(self, segs, kc):
            self.segs = segs
            self.kc = kc
            self.ncols = sum(s.shape[2] for s in segs)
            self.slot = None

        def load(self):
            s = wctr[0] % NSLOT
            wctr[0] += 1
            self.slot = s
            view = wslot[s][:, 0:self.kc * self.ncols].rearrange("p (k n) -> p k n", k=self.kc)
            off = 0
            for sg in self.segs:
                n = sg.shape[2]
                P.dma("pool", (lambda e, sg=sg, dst=view[:, :, off:off + n]: e.dma_start(out=dst, in_=sg)),
                      writes=[Bslot[s]])
                off += n
            self.view = view
            self.buf = Bslot[s]

    steps = []

    def add_piece(segs, kc, consume):
        steps.append(("piece", Piece(segs, kc), consume))

    def add_work(fn):
        steps.append(("work", None, fn))

    dyn_cache = {}

    def dynoff(e, a, b):
        en = str(e.engine)
        if (en, "rank") not in dyn_cache:
            dyn_cache[(en, "rank")] = e.partition_id() % 2
        k = (en, a, b)
        if k not in dyn_cache:
            dyn_cache[k] = e.snap((a + b * dyn_cache[(en, "rank")]) * 128)
        return dyn_cache[k]

    def mm(out_ap, lhsT, rhs, start, stop, reads, wbuf):
        P.op("pe", lambda e: e.matmul(out_ap, lhsT=lhsT, rhs=rhs, start=start, stop=stop),
             reads=reads, writes=[wbuf])

    def init_phase():
        P.op("pool", lambda e: e.memset(ones_b[:], 1.0), writes=[Bconst])
        P.op("pool", lambda e: e.memset(ones_f[:], 1.0), writes=[Bconst])
        P.op("pool", lambda e: e.memset(zero_b[:], 0.0), writes=[Bconst])
        P.op("pool", lambda e: e.memset(ident_b[:], 0.0), writes=[Bconst])
        P.op("pool", lambda e: e.affine_select(out=ident_b[:], in_=ident_b[:], compare_op=ALU.not_equal, fill=1.0,
                                               base=0, pattern=[[-1, 128]], channel_multiplier=1),
             reads=[Bconst], writes=[Bconst])
        P.op("pool", lambda e: e.memset(ident_f[:], 0.0), writes=[Bconst])
        P.op("pool", lambda e: e.affine_select(out=ident_f[:], in_=ident_f[:], compare_op=ALU.not_equal, fill=1.0,
                                               base=0, pattern=[[-1, 128]], channel_multiplier=1),
             reads=[Bconst], writes=[Bconst])
        P.dma("sp", lambda e: e.dma_start(out=maskb[:], in_=maskb_d), writes=[Bparams])
        P.dma("sp", lambda e: e.dma_start(out=hmask[:], in_=hmask_d), writes=[Bparams])
        NCD = dict(allow_slow_non_contiguous=True)
        for l in range(DEPTH):
            P.dma("sp", lambda e, l=l: e.dma_start(out=nw[:, l, :], in_=mixn_d[l].rearrange("(c p) -> p c", p=128), **NCD),
                  writes=[Bparams])
            P.dma("sp", lambda e, l=l: e.dma_start(out=nw[:, DEPTH + l, :], in_=ffnn_d[l].rearrange("(c p) -> p c", p=128), **NCD),
                  writes=[Bparams])
            for k in range(3):
                P.dma("sp", lambda e, l=l, k=k: e.dma_start(out=cw[:, l, k, :],
                                                            in_=convw_d[l, k].rearrange("(c p) -> p c", p=128), **NCD),
                      writes=[Bparams])
        P.dma("sp", lambda e: e.dma_start(out=nw[:, 2 * DEPTH, :], in_=fin_d.rearrange("(c p) -> p c", p=128), **NCD),
              writes=[Bparams])
        P.dma("sp", lambda e: e.dma_start(out=subs[:], in_=subw_d.rearrange("l d -> d l"), **NCD), writes=[Bparams])
        lt = [P.sbuf(f"lam{i}", [128, DEPTH, 64], F32) for i in range(4)]
        Blam = Buf("lam")
        for i, d_ in enumerate([lq1_d, lk1_d, lq2_d, lk2_d]):
            P.dma("sp", lambda e, i=i, d_=d_: e.dma_start(out=lt[i][:], in_=d_.partition_broadcast(128)), writes=[Blam])
        ls = P.sbuf("lsum", [128, 2, DEPTH], F32)
        P.op("dve", lambda e: e.tensor_tensor(out=lt[0][:], in0=lt[0][:], in1=lt[1][:], op=ALU.mult), reads=[Blam], writes=[Blam])
        P.op("dve", lambda e: e.tensor_tensor(out=lt[2][:], in0=lt[2][:], in1=lt[3][:], op=ALU.mult), reads=[Blam], writes=[Blam])
        P.op("dve", lambda e: e.reduce_sum(out=ls[:, 0, :], in_=lt[0][:], axis=AX.X), reads=[Blam], writes=[Blam])
        P.op("dve", lambda e: e.reduce_sum(out=ls[:, 1, :], in_=lt[2][:], axis=AX.X), reads=[Blam], writes=[Blam])
        P.op("act", lambda e: e.activation(out=ls[:], in_=ls[:], func=AF.Exp), reads=[Blam], writes=[Blam])
        P.op("dve", lambda e: e.tensor_tensor(out=neglam[:], in0=ls[:, 1, :], in1=ls[:, 0, :], op=ALU.subtract),
             reads=[Blam], writes=[Bparams])
        for l in range(DEPTH):
            li = 0.8 - 0.6 * math.exp(-0.3 * l)
            P.op("dve", lambda e, l=l, li=li: e.tensor_scalar(out=neglam[:, l:l + 1], in0=neglam[:, l:l + 1], scalar1=-li,
                                                              scalar2=None, op0=ALU.add), reads=[Bparams], writes=[Bparams])
            P.op("dve", lambda e, l=l, li=li: e.tensor_scalar(out=subs[:, l:l + 1], in0=subs[:, l:l + 1], scalar1=1.0 - li,
                                                              scalar2=None, op0=ALU.mult), reads=[Bparams], writes=[Bparams])
        posf = P.sbuf("posf", [128, NTT], F32)
        ang = P.sbuf("ang", [128, NTT, 8], F32)
        tq = P.sbuf("tq", [128, NTT, 8], F32)
        rn = P.sbuf("rn", [128, NTT, 8], F32)
        P.dma("pool", lambda e: e.dma_start(out=posf[:], in_=pos_d), writes=[Brope])
        for j in range(8):
            fr = float(np.float32(THETA) ** np.float32(-(2.0 * j) / 16.0))
            P.op("dve", lambda e, j=j, fr=fr: e.tensor_scalar(out=ang[:, :, j], in0=posf[:], scalar1=fr, scalar2=None,
                                                              op0=ALU.mult), reads=[Brope], writes=[Brope])
        P.op("dve", lambda e: e.tensor_scalar(out=ang[:], in0=ang[:], scalar1=1.0 / (2 * math.pi), scalar2=None,
                                              op0=ALU.mult), reads=[Brope], writes=[Brope])
        for which in range(2):
            if which == 1:
                P.op("dve", lambda e: e.tensor_scalar(out=ang[:], in0=ang[:], scalar1=0.25, scalar2=None, op0=ALU.add),
                     reads=[Brope], writes=[Brope])
            P.op("dve", lambda e: e.tensor_scalar(out=rn[:], in0=ang[:], scalar1=MAGIC, scalar2=None, op0=ALU.add),
                 reads=[Brope], writes=[Brope])
            P.op("dve", lambda e: e.tensor_scalar(out=rn[:], in0=rn[:], scalar1=-MAGIC, scalar2=None, op0=ALU.add),
                 reads=[Brope], writes=[Brope])
            P.op("dve", lambda e: e.tensor_tensor(out=tq[:], in0=ang[:], in1=rn[:], op=ALU.subtract),
                 reads=[Brope], writes=[Brope])
            if which == 0:
                P.op("act", lambda e: e.activation(out=ropeS[:, :, 8:16], in_=tq[:], func=AF.Sin, scale=2 * math.pi),
                     reads=[Brope], writes=[Brope])
                P.op("act", lambda e: e.activation(out=ropeS[:, :, 0:8], in_=tq[:], func=AF.Sin, scale=-2 * math.pi),
                     reads=[Brope], writes=[Brope])
            else:
                P.op("act", lambda e: e.activation(out=ropeC[:, :, 0:8], in_=tq[:], func=AF.Sin, scale=2 * math.pi),
                     reads=[Brope], writes=[Brope])
                P.op("act", lambda e: e.activation(out=ropeC[:, :, 8:16], in_=tq[:], func=AF.Sin, scale=2 * math.pi),
                     reads=[Brope], writes=[Brope])
        switch("Z", Bxstage)
        for tt in range(NTT):
            s = tt % 2
            P.dma("sp", lambda e, tt=tt, s=s: e.dma_start(out=xstage[s], in_=x_d[tt * 128:(tt + 1) * 128, :]),
                  writes=[Bxstage[s]])
            tb = tt // 4
            for c4 in range(2):
                bk, Bbk = nb()
                for i in range(4):
                    c = c4 * 4 + i
                    P.op("pe", lambda e, bk=bk, i=i, c=c, s=s: e.transpose(out=bk[:, i * 128:(i + 1) * 128],
                                                                        in_=xstage[s][:, c * 128:(c + 1) * 128],
                                                                        identity=ident_f[:]),
                         reads=[Bxstage[s], Bconst], writes=[Bbk])
                dst = xT[:, c4 * 4:(c4 + 1) * 4, tt * 128:(tt + 1) * 128]
                src = bk[:].rearrange("p (c t) -> p c t", c=4)
                eng = "dve" if c4 == 0 else "act"
                if eng == "dve":
                    P.op("dve", lambda e, dst=dst, src=src: e.tensor_copy(out=dst, in_=src), reads=[Bbk],
                         writes=[BxT[c4 * 4 + i][tb] for i in range(4)])
                else:
                    P.op("act", lambda e, dst=dst, src=src: e.copy(out=dst, in_=src), reads=[Bbk],
                         writes=[BxT[c4 * 4 + i][tb] for i in range(4)])

    def norm_phase(widx):
        for tb in range(NTB):
            ts = slice(tb * 512, (tb + 1) * 512)
            bk, Bbk = nb()
            for c in range(8):
                sq, Bsq = ft()
                sqb = sq.bitcast(BF16)[:, 0:512]
                P.op("dve" if c % 2 == 0 else "pool", lambda e, sqb=sqb, c=c, ts=ts: e.tensor_tensor(
                    out=sqb, in0=xT[:, c, ts], in1=xT[:, c, ts], op=ALU.mult),
                     reads=[BxT[c][tb]], writes=[Bsq])
                mm(bk[:], ones_b[:], sqb, c == 0, c == 7, [Bsq, Bconst], Bbk)
            rs, Brs = ft()
            P.op("act", lambda e, rs=rs, bk=bk: e.activation(out=rs[:], in_=bk[:], func=AF.Ln, bias=EPS, scale=1.0 / D),
                 reads=[Bbk], writes=[Brs])
            P.op("act", lambda e, rs=rs: e.activation(out=rs[:], in_=rs[:], func=AF.Exp, scale=-0.5),
                 reads=[Brs], writes=[Brs])
            for c in range(8):
                P.op("dve", lambda e, c=c, ts=ts, rs=rs: e.scalar_tensor_tensor(
                    out=xnT[:, c, ts], in0=xT[:, c, ts], scalar=nw[:, widx, c:c + 1], in1=rs[:],
                    op0=ALU.mult, op1=ALU.mult),
                     reads=[BxT[c][tb], Brs, Bparams], writes=[BxnT[tb]])

    def proj(bk, Bbk, pv, pbuf, col0, rhs_fn, rhs_bufs, nk):
        for kc in range(nk):
            mm(bk[:], pv[:, kc, col0:col0 + 128], rhs_fn(kc), kc == 0, kc == nk - 1, [pbuf] + rhs_bufs, Bbk)

    def branch_b(l):
        W = kview(win_d[l])

        def d1(c):
            def consume(pc):
                for tb in range(NTB):
                    ts = slice(tb * 512, (tb + 1) * 512)
                    b0, Bb0 = nb()
                    b1, Bb1 = nb()
                    proj(b0, Bb0, pc.view, pc.buf, 0, lambda kc: xnT[:, kc, ts], [BxnT[tb]], 8)
                    proj(b1, Bb1, pc.view, pc.buf, 128, lambda kc: xnT[:, kc, ts], [BxnT[tb]], 8)
                    t, Bt = ft()
                    P.op("act", lambda e, t=t, b0=b0: e.copy(out=t[:], in_=b0[:]), reads=[Bb0], writes=[Bt])
                    P.op("dve", lambda e, t=t, b1=b1, c=c, ts=ts: e.tensor_tensor(out=zT[:, c, ts], in0=b1[:], in1=t[:],
                                                                                  op=ALU.mult),
                         reads=[Bb1, Bt], writes=[BzT[c][tb]])
            return consume

        for c in range(8):
            add_piece([W[:, :, 4096 + c * 128:4096 + (c + 1) * 128], W[:, :, 5120 + c * 128:5120 + (c + 1) * 128]], 8, d1(c))

        def halo():
            for j in range(NTB):
                P.dma("sp", lambda e, j=j: e.dma_start(
                    out=halm[j * 128:(j + 1) * 128, :].rearrange("p (c t) -> p c t", c=8),
                    in_=zT[:, :, j * 512 + 510:j * 512 + 512]),
                      reads=[BzT[c][j] for c in range(8)], writes=[Bhalm])
            P.dma("pool", lambda e: e.collective_compute("AllGather", ALU.bypass, replica_groups=GROUPS,
                                                         ins=[halm[:, :]], outs=[halg[:, :]]),
                  reads=[Bhalm], writes=[Bhalg], inc_amt=1)
            coef = [(0, 0), (5, -1), (1, 1), (7, -1)]
            for j in range(NTB):
                a, b = coef[j]
                P.dma("sp", lambda e, j=j, a=a, b=b: e.dma_start(
                    out=zh[:, j, :],
                    in_=halg[bass.ds(dynoff(e, a, b), 128), :]),
                      reads=[Bhalg], writes=[Bzh])
            P.op("pool", lambda e: e.tensor_tensor(out=zh[:], in0=zh[:], in1=hmask[:, :].unsqueeze(2).to_broadcast([128, NTB, 16]),
                                                   op=ALU.mult), reads=[Bzh, Bparams], writes=[Bzh])
        add_work(halo)

        def d3(c):
            def consume(pc):
                P.op("pool", lambda e: e.tensor_copy(out=zx[:, :, 2:514],
                                                     in_=zT[:, c, :].rearrange("p (j t) -> p j t", j=4)),
                     reads=BzT[c], writes=[Bzx])
                P.op("pool", lambda e: e.tensor_copy(out=zx[:, :, 0:2],
                                                     in_=zh[:, :, 2 * c:2 * c + 2]),
                     reads=[Bzh], writes=[Bzx])
                cv = convtmp.rearrange("p (j t) -> p j t", j=4)
                P.op("dve", lambda e: e.tensor_scalar(out=cv, in0=zx[:, :, 2:514], scalar1=cw[:, l, 2, c:c + 1],
                                                      scalar2=None, op0=ALU.mult),
                     reads=[Bzx, Bparams], writes=[Bconv])
                P.op("dve", lambda e: e.scalar_tensor_tensor(out=cv, in0=zx[:, :, 1:513], scalar=cw[:, l, 1, c:c + 1],
                                                             in1=cv, op0=ALU.mult, op1=ALU.add),
                     reads=[Bzx, Bparams, Bconv], writes=[Bconv])
                P.op("dve", lambda e: e.scalar_tensor_tensor(out=cv, in0=zx[:, :, 0:512], scalar=cw[:, l, 0, c:c + 1],
                                                             in1=cv, op0=ALU.mult, op1=ALU.add),
                     reads=[Bzx, Bparams, Bconv], writes=[Bconv])
                for tb in range(NTB):
                    ts = slice(tb * 512, (tb + 1) * 512)
                    b0, Bb0 = nb()
                    proj(b0, Bb0, pc.view, pc.buf, 0, lambda kc: xnT[:, kc, ts], [BxnT[tb]], 8)
                    P.op("dve", lambda e, b0=b0, ts=ts: e.tensor_tensor(out=zT[:, c, ts], in0=b0[:], in1=convtmp[:, ts],
                                                                        op=ALU.mult),
                         reads=[Bb0, Bconv], writes=[BzT[c][tb]])
            return consume

        for c in range(8):
            add_piece([W[:, :, 3072 + c * 128:3072 + (c + 1) * 128]], 8, d3(c))

        Wb = kview(wb_d[l])

        def d4(n):
            def consume(pc):
                for tb in range(NTB):
                    ts = slice(tb * 512, (tb + 1) * 512)
                    b0, Bb0 = nb()
                    b1, Bb1 = nb()
                    proj(b0, Bb0, pc.view, pc.buf, 0, lambda kc: zT[:, kc, ts], [BzT[kc][tb] for kc in range(8)], 8)
                    proj(b1, Bb1, pc.view, pc.buf, 128, lambda kc: xnT[:, kc, ts], [BxnT[tb]], 8)
                    t, Bt = ft()
                    P.op("act", lambda e, t=t, b1=b1: e.activation(out=t[:], in_=b1[:], func=AF.Sigmoid),
                         reads=[Bb1], writes=[Bt])
                    si = (n * NTB + tb) % 3
                    P.op("dve", lambda e, t=t, b0=b0, si=si: e.tensor_tensor(out=gbst4[si], in0=b0[:], in1=t[:], op=ALU.mult),
                         reads=[Bb0, Bt], writes=[Bgbst4[si]])
                    P.dma("sp", lambda e, si=si, n=n, ts=ts: e.dma_start(out=gb_dram[n * 128:(n + 1) * 128, ts], in_=gbst4[si]),
                          reads=[Bgbst4[si]], writes=[Bgb_d[n]])
            return consume

        for n in range(8):
            add_piece([Wb[:, :, n * 128:(n + 1) * 128], W[:, :, 7168 + n * 128:7168 + (n + 1) * 128]], 8, d4(n))

    def qkv_head(l, h):
        W = kview(win_d[l])
        s = h % 2

        def consume(pc):
            tpb = [None]

            def do_mm_rope(tt):
                bk, Bbk = nb()
                for kc in range(8):
                    mm(bk[:, 0:384], xnT[:, kc, tt * 128:(tt + 1) * 128], pc.view[:, kc, :], kc == 0, kc == 7,
                       [pc.buf, BxnT[tt // 4]], Bbk)
                qs = tt % 2
                qk, Bqk = qkbf[qs], Bqkbf[qs]
                r1, r2, Br = rt1[qs], rt2[qs], Brt[qs]
                g4 = bk[:, 0:256].rearrange("p (g d) -> p g d", g=4)
                P.op("dve", lambda e, qk=qk, bk=bk: e.tensor_copy(out=qk, in_=bk[:, 0:256]), reads=[Bbk], writes=[Bqk])
                P.op("dve", lambda e, r1=r1, g4=g4, tt=tt: e.tensor_tensor(
                    out=r1[:], in0=g4[:, :, 0:16], in1=ropeC[:, tt:tt + 1, :].to_broadcast([128, 4, 16]), op=ALU.mult),
                     reads=[Bbk, Brope], writes=[Br])
                P.op("dve", lambda e, r2=r2, g4=g4, tt=tt: e.tensor_tensor(
                    out=r2[:, :, 0:8], in0=g4[:, :, 8:16], in1=ropeS[:, tt:tt + 1, 0:8].to_broadcast([128, 4, 8]), op=ALU.mult),
                     reads=[Bbk, Brope], writes=[Br])
                P.op("dve", lambda e, r2=r2, g4=g4, tt=tt: e.tensor_tensor(
                    out=r2[:, :, 8:16], in0=g4[:, :, 0:8], in1=ropeS[:, tt:tt + 1, 8:16].to_broadcast([128, 4, 8]), op=ALU.mult),
                     reads=[Bbk, Brope], writes=[Br])
                P.op("dve", lambda e, bk=bk, tt=tt: e.tensor_copy(out=mkv[s][:, 2048 + tt * 128:2048 + (tt + 1) * 128],
                                                                  in_=bk[:, 256:384]),
                     reads=[Bbk], writes=[Bmkv[s]])
                P.op("dve", lambda e, qk=qk, r1=r1, r2=r2: e.tensor_tensor(
                    out=qk.rearrange("p (g d) -> p g d", g=4)[:, :, 0:16], in0=r1[:], in1=r2[:], op=ALU.add),
                     reads=[Br], writes=[Bqk])

            def do_tr(tt):
                qs = tt % 2
                qk, Bqk = qkbf[qs], Bqkbf[qs]
                if tt % 4 == 0:
                    tpb[0] = nb()
                tb_, Btb = tpb[0]
                tv = tb_[:].bitcast(BF16)
                i4 = tt % 4
                P.op("pe", lambda e, tv=tv, qk=qk, i4=i4: e.transpose(out=tv[:, i4 * 128:(i4 + 1) * 128], in_=qk[:, 0:128],
                                                                      identity=ident_b[:]),
                     reads=[Bqk, Bconst], writes=[Btb])
                P.op("pe", lambda e, tv=tv, qk=qk, i4=i4: e.transpose(out=tv[:, 512 + i4 * 128:512 + (i4 + 1) * 128],
                                                                      in_=qk[:, 128:256], identity=ident_b[:]),
                     reads=[Bqk, Bconst], writes=[Btb])
                if tt % 4 == 3:
                    t0 = (tt // 4) * 512
                    P.op("dve", lambda e, tv=tv, t0=t0: e.tensor_copy(out=qT[s][:, t0:t0 + 512], in_=tv[:, 0:512]),
                         reads=[Btb], writes=[BqT[s]])
                    P.op("dve", lambda e, tv=tv, t0=t0: e.tensor_copy(out=mkv[s][:, t0:t0 + 512], in_=tv[:, 512:1024]),
                         reads=[Btb], writes=[Bmkv[s]])

            for tt in range(NTT):
                do_mm_rope(tt)
                if tt > 0:
                    do_tr(tt - 1)
            do_tr(NTT - 1)
            P.dma("sp", lambda e: e.dma_start(out=kvm[s][:, :], in_=mkv[s]), reads=[Bmkv[s]], writes=[Bkvm[s]])
            P.dma("pool", lambda e: e.collective_compute("AllGather", ALU.bypass, replica_groups=GROUPS,
                                                         ins=[kvm[s][:, :]], outs=[kvg[s][:, :]]),
                  reads=[Bkvm[s]], writes=[Bkvg[s]], inc_amt=1)
            P.dma("pool", lambda e: e.dma_start(out=okv[s], in_=kvg[s][bass.ds(dynoff(e, 1, -1), 128), :]),
                  reads=[Bkvg[s]], writes=[Bokv[s]])

        add_piece([W[:, :, h * 128:(h + 1) * 128], W[:, :, 1024 + h * 128:1024 + (h + 1) * 128],
                   W[:, :, 2048 + h * 128:2048 + (h + 1) * 128]], 8, consume)

    pend = {"A": None, "B": None, "S": None}
    sctr = [0]

    def misc_banks(pair):
        if pair is None:
            return [nb(), nb()]
        return [(banks[2 * pair], Bbank[2 * pair]), (banks[2 * pair + 1], Bbank[2 * pair + 1])]

    def flush_pending(pair=None):
        if pend["S"] is not None:
            f = pend["S"]
            pend["S"] = None
            f()
        if pend["A"] is not None:
            f = pend["A"]
            pend["A"] = None
            f(pair)
        if pend["B"] is not None:
            f = pend["B"]
            pend["B"] = None
            f(pair)

    def attention_head(l, h):
        s = h % 2
        pctr = [0]
        O1, BO1 = banks[4], Bbank[4]
        O2, BO2 = banks[5], Bbank[5]
        S1, BS1 = banks[6], Bbank[6]
        S2, BS2 = banks[7], Bbank[7]
        O12 = psall[:, 4 * 512:6 * 512].rearrange("p (m q) -> p m q", m=2)
        S12 = psall[:, 6 * 512:8 * 512].rearrange("p (m q) -> p m q", m=2)
        for j in range(NTB):
            tiles = []
            for lb in range(j):
                for r in range(4):
                    tiles.append(("m", 4 * lb + r, 0, None))
                for r in range(4):
                    tiles.append(("o", 4 * lb + r, 0, None))
            for r in range(4):
                tiles.append(("o", 4 * j + r, 0, "maybe"))
            for r in range(4):
                tiles.append(("m", 4 * j + r, 128 * r, "diag"))
            nt = len(tiles)
            nq = nt // 4
            q0 = j * 512

            def scores(i):
                src, kt, c0, kind = tiles[i]
                kvb, Bkvb = (mkv[s], Bmkv[s]) if src == "m" else (okv[s], Bokv[s])
                pair = sctr[0] % 2
                sctr[0] += 1
                sa, Bsa = banks[2 * pair], Bbank[2 * pair]
                sb_, Bsb = banks[2 * pair + 1], Bbank[2 * pair + 1]
                mm(sa[:, c0:512], kvb[0:64, kt * 128:(kt + 1) * 128], qT[s][0:64, q0 + c0:q0 + 512], True, True,
                   [Bkvb, BqT[s]], Bsa)
                mm(sb_[:, c0:512], kvb[64:128, kt * 128:(kt + 1) * 128], qT[s][64:128, q0 + c0:q0 + 512], True, True,
                   [Bkvb, BqT[s]], Bsb)
                return (sa, Bsa, sb_, Bsb, pair)

            def padd(dst, src, c0=0):
                P.op("dve", lambda e: e.tensor_tensor(out=pp[dst][:, :, c0:512], in0=pp[dst][:, :, c0:512],
                                                      in1=pp[src][:, :, c0:512], op=ALU.add),
                     reads=[Bp1[dst], Bp1[src], Bp2[dst], Bp2[src]], writes=[Bp1[dst], Bp2[dst]])

            def make_sums(nq_):
                q_, n_ = [], [0]

                def push(at, yb):
                    q_.append((at, yb))

                def emit(upto):
                    while q_ and q_[0][0] <= upto:
                        _, yb = q_.pop(0)
                        st_, sp2_ = (n_[0] == 0), (n_[0] == nq_ - 1)
                        n_[0] += 1
                        mm(S1[:], ones_b[:], pp[yb][:, 0, :], st_, sp2_, [Bconst, Bp1[yb]], BS1)
                        mm(S2[:], ones_b[:], pp[yb][:, 1, :], st_, sp2_, [Bconst, Bp2[yb]], BS2)
                return push, emit

            push_sum, emit_sums = make_sums(nq)

            scq = [scores(0), scores(1)]
            if pend["S"] is not None:
                f = pend["S"]
                pend["S"] = None
                f()
            quad = []
            for i in range(nt):
                src, kt, c0, kind = tiles[i]
                kvb, Bkvb = (mkv[s], Bmkv[s]) if src == "m" else (okv[s], Bokv[s])
                sa, Bsa, sb_, Bsb, cur_pair = scq.pop(0)
                y = pctr[0] % NPB
                pctr[0] += 1
                bias = maskb[:, j:j + 1] if kind == "maybe" else 0.0
                rb = [Bparams] if kind == "maybe" else []
                s2 = psall[:, (2 * cur_pair) * 512:(2 * cur_pair + 2) * 512].rearrange("p (m q) -> p m q", m=2)
                P.op("act", lambda e, y=y, s2=s2, c0=c0, bias=bias: e.activation(out=pp[y][:, :, c0:512], in_=s2[:, :, c0:512],
                                                                                func=AF.Exp, bias=bias, scale=0.125),
                     reads=[Bsa, Bsb] + rb, writes=[Bp1[y], Bp2[y]])
                if kind == "diag":
                    P.op("dve", lambda e, y=y, c0=c0: e.memset(pp[y][64:128, :, c0:c0 + 64], 0.0), reads=[],
                         writes=[Bp1[y], Bp2[y]])
                if i == 2 and pend["A"] is not None:
                    f = pend["A"]
                    pend["A"] = None
                    f(cur_pair)
                if i == 7 and pend["B"] is not None:
                    f = pend["B"]
                    pend["B"] = None
                    f(cur_pair)
                if i + 2 < nt:
                    scq.append(scores(i + 2))
                vt = kvb[:, 2048 + kt * 128:2048 + (kt + 1) * 128]
                st, sp_ = (i == 0), (i == nt - 1)
                mm(O1[:, c0:512], vt, p1[y][:, c0:512], st, sp_, [Bkvb, Bp1[y]], BO1)
                mm(O2[:, c0:512], vt, p2[y][:, c0:512], st, sp_, [Bkvb, Bp2[y]], BO2)
                ph = i % 4
                if ph == 0:
                    quad = [(y, c0)]
                else:
                    quad.append((y, c0))
                if ph == 1:
                    padd(quad[0][0], quad[1][0], quad[1][1])
                if ph == 3:
                    padd(quad[2][0], quad[3][0], quad[3][1])
                    padd(quad[0][0], quad[2][0], quad[2][1])
                    push_sum(i + 2, quad[0][0])
                emit_sums(i)
            flush_pending()
            pend["S"] = (lambda es=emit_sums: es(10 ** 9))
            rr, Brr = ft2()
            ob, Bob = osb[0], Bosb[0]
            P.op("dve", lambda e, ob=ob: e.tensor_copy(out=ob, in_=O12), reads=[BO1, BO2], writes=[Bob])

            def stage_a(pair, rr=rr, Brr=Brr, ob=ob, Bob=Bob, j=j, q0=q0):
                P.op("act", lambda e: e.activation(out=rr, in_=S12, func=AF.Ln), reads=[BS1, BS2, Bob], writes=Brr)
                P.op("act", lambda e: e.activation(out=rr, in_=rr, func=AF.Exp, scale=-1.0), reads=Brr, writes=Brr)
                P.op("dve", lambda e: e.tensor_tensor(out=rr, in0=ob, in1=rr, op=ALU.mult),
                     reads=[Bob] + Brr, writes=Brr)
                o_, Bo = ft()
                P.op("dve", lambda e: e.scalar_tensor_tensor(out=o_[:], in0=rr[:, 1, :], scalar=neglam[:, l:l + 1],
                                                             in1=rr[:, 0, :], op0=ALU.mult, op1=ALU.add),
                     reads=Brr + [Bparams], writes=[Bo])
                sq, Bsq = ft()
                sqb = sq.bitcast(BF16)[:, 0:512]
                P.op("dve", lambda e: e.tensor_tensor(out=sqb, in0=o_[:], in1=o_[:], op=ALU.mult),
                     reads=[Bo], writes=[Bsq])

                def stage_b(pair):
                    bq, Bbq = misc_banks(pair)[0]
                    mm(bq[:], ones_b[:], sqb, True, True, [Bsq, Bconst], Bbq)
                    rs, Brs = ft()
                    P.op("act", lambda e: e.activation(out=rs[:], in_=bq[:], func=AF.Ln, bias=EPS, scale=1.0 / 128),
                         reads=[Bbq], writes=[Brs])
                    P.op("act", lambda e: e.activation(out=rs[:], in_=rs[:], func=AF.Exp, scale=-0.5),
                         reads=[Brs], writes=[Brs])
                    so = (h * NTB + j) % 2
                    P.op("dve", lambda e: e.scalar_tensor_tensor(out=onst[so], in0=o_[:], scalar=subs[:, l:l + 1],
                                                                 in1=rs[:], op0=ALU.mult, op1=ALU.mult),
                         reads=[Bo, Brs, Bparams], writes=[Bonst[so]])
                    P.dma("sp", lambda e: e.dma_start(out=on_dram[h * 128:(h + 1) * 128, q0:q0 + 512], in_=onst[so]),
                          reads=[Bonst[so]], writes=[Bon_d[h]])
                pend["B"] = stage_b

            pend["A"] = stage_a

    def mix_out(l):
        W = kview(win_d[l])
        Wa = kview(wa_d[l])
        Wo = kview(wo_d[l])

        def load_on():
            switch("Z", flat(BzT))
            switch("K", flat(BmixT))
            for h in range(NH):
                P.dma("sp", lambda e, h=h: e.dma_start(out=zT[:, h, :], in_=on_dram[h * 128:(h + 1) * 128, :]),
                      reads=[Bon_d[h]], writes=BzT[h])
        add_work(load_on)

        def d5(n):
            def consume(pc):
                for tb in range(NTB):
                    ts = slice(tb * 512, (tb + 1) * 512)
                    gi = (n * NTB + tb) % 2
                    P.dma("sp", lambda e, gi=gi, ts=ts: e.dma_start(out=gbld[gi][:], in_=gb_dram[n * 128:(n + 1) * 128, ts]),
                          reads=[Bgb_d[n]], writes=[Bgbld[gi]])
                    b0, Bb0 = nb()
                    b1, Bb1 = nb()
                    proj(b0, Bb0, pc.view, pc.buf, 0, lambda kc: zT[:, kc, ts], [BzT[kc][tb] for kc in range(8)], 8)
                    proj(b1, Bb1, pc.view, pc.buf, 128, lambda kc: xnT[:, kc, ts], [BxnT[tb]], 8)
                    t, Bt = ft()
                    P.op("act", lambda e, t=t, b1=b1: e.activation(out=t[:], in_=b1[:], func=AF.Sigmoid),
                         reads=[Bb1], writes=[Bt])
                    P.op("dve", lambda e, t=t, b0=b0: e.tensor_tensor(out=t[:], in0=b0[:], in1=t[:], op=ALU.mult),
                         reads=[Bb0, Bt], writes=[Bt])
                    P.op("pool", lambda e, t=t, gi=gi, ts=ts: e.tensor_tensor(out=mixT[:, n, ts], in0=t[:], in1=gbld[gi][:],
                                                                              op=ALU.add),
                         reads=[Bt, Bgbld[gi]], writes=[BmixT[n][tb]])
            return consume

        for n in range(8):
            add_piece([Wa[:, :, n * 128:(n + 1) * 128], W[:, :, 6144 + n * 128:6144 + (n + 1) * 128]], 8, d5(n))

        def d6(n):
            def consume(pc):
                for tb in range(NTB):
                    ts = slice(tb * 512, (tb + 1) * 512)
                    b0, Bb0 = nb()
                    proj(b0, Bb0, pc.view, pc.buf, 0, lambda kc: mixT[:, kc, ts], [BmixT[kc][tb] for kc in range(8)], 8)
                    P.op("dve", lambda e, b0=b0, ts=ts: e.tensor_tensor(out=xT[:, n, ts], in0=xT[:, n, ts], in1=b0[:],
                                                                        op=ALU.add),
                         reads=[Bb0, BxT[n][tb]], writes=[BxT[n][tb]])
            return consume

        for n in range(8):
            add_piece([Wo[:, :, n * 128:(n + 1) * 128]], 8, d6(n))

    def ffn(l):
        Wg = kview(wg_d[l])
        Wu = kview(wu_d[l])
        Wd = wd_d[l].rearrange("(fc fp) n -> fp fc n", fp=128)
        add_work(lambda: norm_phase(DEPTH + l))
        for half in range(2):
            def sw():
                switch("Z", flat(BhT))
                switch("K", [])
            if half == 0:
                def sw0():
                    ops = retire(region["Z"]) + retire(region["K"])
                    inherit(flat(BhT), ops)
                    region["Z"] = flat(BhT)
                    region["K"] = []
                add_work(sw0)

            def gu(fc, fl):
                def consume(pc):
                    for tb in range(NTB):
                        ts = slice(tb * 512, (tb + 1) * 512)
                        b0, Bb0 = nb()
                        b1, Bb1 = nb()
                        proj(b0, Bb0, pc.view, pc.buf, 0, lambda kc: xnT[:, kc, ts], [BxnT[tb]], 8)
                        proj(b1, Bb1, pc.view, pc.buf, 128, lambda kc: xnT[:, kc, ts], [BxnT[tb]], 8)
                        t, Bt = ft()
                        P.op("act", lambda e, t=t, b0=b0: e.activation(out=t[:], in_=b0[:], func=AF.Silu),
                             reads=[Bb0], writes=[Bt])
                        P.op("dve", lambda e, t=t, b1=b1, ts=ts: e.tensor_tensor(out=hT[:, fl, ts], in0=b1[:], in1=t[:],
                                                                                 op=ALU.mult),
                             reads=[Bb1, Bt], writes=[BhT[fl][tb]])
                return consume

            for fl in range(11):
                fc = half * 11 + fl
                add_piece([Wg[:, :, fc * 128:(fc + 1) * 128], Wu[:, :, fc * 128:(fc + 1) * 128]], 8, gu(fc, fl))

            def dn(n):
                def consume(pc):
                    for tb in range(NTB):
                        ts = slice(tb * 512, (tb + 1) * 512)
                        b0, Bb0 = nb()
                        proj(b0, Bb0, pc.view, pc.buf, 0, lambda kc: hT[:, kc, ts], [BhT[kc][tb] for kc in range(11)], 11)
                        P.op("dve", lambda e, b0=b0, ts=ts: e.tensor_tensor(out=xT[:, n, ts], in0=xT[:, n, ts], in1=b0[:],
                                                                            op=ALU.add),
                             reads=[Bb0, BxT[n][tb]], writes=[BxT[n][tb]])
                return consume

            for n in range(8):
                add_piece([Wd[:, half * 11:(half + 1) * 11, n * 128:(n + 1) * 128]], 11, dn(n))

    def final_phase():
        ops = retire(region["Z"]) + retire(region["K"])
        inherit(Bxstage, ops)
        region["Z"] = list(Bxstage)
        region["K"] = []
        outs = []
        for tb in range(NTB):
            ts = slice(tb * 512, (tb + 1) * 512)
            bk, Bbk = nb()
            for c in range(8):
                sq, Bsq = ft()
                sqb = sq.bitcast(BF16)[:, 0:512]
                P.op("dve" if c % 2 == 0 else "pool", lambda e, sqb=sqb, c=c, ts=ts: e.tensor_tensor(
                    out=sqb, in0=xT[:, c, ts], in1=xT[:, c, ts], op=ALU.mult),
                     reads=[BxT[c][tb]], writes=[Bsq])
                mm(bk[:], ones_b[:], sqb, c == 0, c == 7, [Bsq, Bconst], Bbk)
            rs, Brs = ft()
            P.op("act", lambda e, rs=rs, bk=bk: e.activation(out=rs[:], in_=bk[:], func=AF.Ln, bias=EPS, scale=1.0 / D),
                 reads=[Bbk], writes=[Brs])
            P.op("act", lambda e, rs=rs: e.activation(out=rs[:], in_=rs[:], func=AF.Exp, scale=-0.5),
                 reads=[Brs], writes=[Brs])
            for c in range(8):
                P.op("dve", lambda e, c=c, ts=ts, rs=rs: e.scalar_tensor_tensor(
                    out=xT[:, c, ts], in0=xT[:, c, ts], scalar=nw[:, 2 * DEPTH, c:c + 1], in1=rs[:],
                    op0=ALU.mult, op1=ALU.mult),
                     reads=[BxT[c][tb], Brs, Bparams], writes=[BxT[c][tb]])
            for t4 in range(4):
                tt = tb * 4 + t4
                s = tt % 2
                for c4 in range(2):
                    bt, Bbt = nb()
                    for i in range(4):
                        c = c4 * 4 + i
                        P.op("pe", lambda e, bt=bt, i=i, c=c, tt=tt: e.transpose(out=bt[:, i * 128:(i + 1) * 128],
                                                                              in_=xT[:, c, tt * 128:(tt + 1) * 128],
                                                                              identity=ident_f[:]),
                             reads=[BxT[c][tb], Bconst], writes=[Bbt])
                    if c4 == 0:
                        P.op("dve", lambda e, bt=bt, s=s: e.tensor_copy(out=xstage[s][:, 0:512], in_=bt[:]),
                             reads=[Bbt], writes=[Bxstage[s]])
                    else:
                        P.op("act", lambda e, bt=bt, s=s: e.copy(out=xstage[s][:, 512:1024], in_=bt[:]),
                             reads=[Bbt], writes=[Bxstage[s]])
                outs.append(P.dma("sp", lambda e, tt=tt, s=s: e.dma_start(out=out_d[tt * 128:(tt + 1) * 128, :], in_=xstage[s]),
                                  reads=[Bxstage[s]], writes=[Buf(f"out{tt}")], sem_buf=Bxstage[s]))
        P.barrier("sp", outs)

    add_work(init_phase)
    for l in range(depth):
        if stop is not None and stop < 1:
            break
        add_work(lambda l=l: norm_phase(l))
        if stop is not None and stop < 2:
            break

        def sw_b():
            ops = retire(region["Z"]) + retire(region["K"])
            inherit(flat(BzT), ops)
            inherit([Bconv, Bzx] + Bgbst4, ops)
            region["Z"] = flat(BzT)
            region["K"] = [Bconv, Bzx] + Bgbst4
        add_work(sw_b)
        branch_b(l)
        if stop is not None and stop < 2.3:
            break

        def sw_att():
            switch("Z", BqT + Bp1 + Bp2 + Bonst + Bqkbf + Bosb)
            switch("K", Bmkv + Bokv)
            bank_mod[0] = 4
        add_work(sw_att)
        qkv_head(l, 0)
        if stop is not None and stop < 2.4:
            break
        qkv_head(l, 1)
        if stop is not None and stop < 2.6:
            break
        for h in range(NH):
            add_work(lambda l=l, h=h: attention_head(l, h))
            if stop is not None and stop < 2.8:
                break
            if h + 2 < NH:
                qkv_head(l, h + 2)
            if stop is not None and stop < 3 and h + 1 >= round((stop - 2.8) * 100):
                break

        def sw_dense():
            flush_pending()
            bank_mod[0] = 8
        add_work(sw_dense)
        if stop is not None and stop < 4:
            break
        mix_out(l)
        if stop is not None and stop < 5:
            break
        ffn(l)
    add_work(final_phase)

    piece_idx = [i for i, st in enumerate(steps) if st[0] == "piece"]
    loaded = [0]

    def load_upto(k):
        while loaded[0] < len(piece_idx) and loaded[0] < k:
            steps[piece_idx[loaded[0]]][1].load()
            loaded[0] += 1

    np_done = 0
    load_upto(NSLOT)
    for st in steps:
        if st[0] == "work":
            st[2]()
        else:
            st[2](st[1])
            np_done += 1
            load_upto(np_done + NSLOT)
    P.emit()
    return nc


def _blocks(p):
    return [2 * j + (p ^ (j & 1)) for j in range(4)]


_NC_CACHE = {}


def kernel(x, positions, mix_norm, w_in, lambda_q1, lambda_k1, lambda_q2, lambda_k2,
           subln_w, conv_w, w_branch_a, w_branch_b, w_out, ffn_norm,
           w_gate, w_up, w_down, final_norm, _depth=DEPTH, _stop=None):
    x = np.asarray(x, dtype=np.float32)
    positions = np.asarray(positions, dtype=np.int32)
    shared = {
        "mix_norm": mix_norm, "w_in": w_in, "lambda_q1": lambda_q1, "lambda_k1": lambda_k1,
        "lambda_q2": lambda_q2, "lambda_k2": lambda_k2, "subln_w": subln_w, "conv_w": conv_w,
        "w_branch_a": w_branch_a, "w_branch_b": w_branch_b, "w_out": w_out, "ffn_norm": ffn_norm,
        "w_gate": w_gate, "w_up": w_up, "w_down": w_down, "final_norm": final_norm,
    }
    shared = {k: np.ascontiguousarray(np.asarray(v, dtype=np.float32)) for k, v in shared.items()}
    if _depth < DEPTH:
        for k in ("w_in", "w_branch_a", "w_branch_b", "w_out", "w_gate", "w_up", "w_down"):
            shared[k] = np.ascontiguousarray(shared[k][:_depth])
    in_maps = []
    for c in range(8):
        b, p = c // 2, c % 2
        idx = np.concatenate([np.arange(512 * g, 512 * (g + 1)) for g in _blocks(p)])
        xs = np.ascontiguousarray(x[b, idx, :])
        ps = np.ascontiguousarray(positions[b, idx].reshape(NTT, 128).T)
        mb = np.zeros((128, NTB), np.float32)
        for j in range(NTB):
            if (p ^ (j & 1)) != 1:
                mb[:, j] = NEG
        hm = np.ones((128, NTB), np.float32)
        if p == 0:
            hm[:, 0] = 0.0
        m = {"x": xs, "pos": ps, "maskb": mb, "hmask": hm}
        m.update(shared)
        in_maps.append(m)
    if (_depth, _stop) not in _NC_CACHE:
        _NC_CACHE[(_depth, _stop)] = build_program(_depth, _stop)
    nc = _NC_CACHE[(_depth, _stop)]
    res = run_bass_kernel_spmd(nc, in_maps, core_ids=list(range(8)))
    out = np.empty((4, 4096, D), np.float32)
    for c in range(8):
        b, p = c // 2, c % 2
        idx = np.concatenate([np.arange(512 * g, 512 * (g + 1)) for g in _blocks(p)])
        out[b, idx, :] = np.asarray(res.results[c]["out"], dtype=np.float32)
    return out
```

```python
import math
import os
from contextlib import ExitStack

import numpy as np
import concourse.bass as bass
import concourse.mybir as mybir
from concourse.bass_utils import run_bass_kernel_spmd

F32 = mybir.dt.float32
BF16 = mybir.dt.bfloat16
I32 = mybir.dt.int32
AF = mybir.ActivationFunctionType
ALU = mybir.AluOpType
AX = mybir.AxisListType

D = 1024
DEPTH = 4
NH = 8
DFF = 2816
NFC = DFF // 128
TOK = 2048
NTB = 4
NTT = 16
EPS = 1e-6
THETA = 500000.0
GROUPS = [[0, 1], [2, 3], [4, 5], [6, 7]]
MAGIC = 12582912.0
NEG = -30000.0

SEM_LIMIT = 2000
DSEM_LIMIT = 2000


class Buf:
    __slots__ = ("name", "last_write", "reads", "dsem", "excl")

    def __init__(self, name, excl=False):
        self.name = name
        self.excl = excl
        self.last_write = None
        self.reads = {}
        self.dsem = None


class DSem:
    __slots__ = ("handle", "count")

    def __init__(self, handle):
        self.handle = handle
        self.count = 0


class Op:
    __slots__ = ("eng", "fn", "deps", "signal", "tok", "is_dma", "dsem", "inc_amt", "seq")
    _ctr = [0]

    def __init__(self, eng, fn):
        Op._ctr[0] += 1
        self.seq = Op._ctr[0]
        self.eng = eng
        self.fn = fn
        self.deps = []
        self.signal = False
        self.tok = None
        self.is_dma = False
        self.dsem = None
        self.inc_amt = 1


def rkey(op):
    return ("d", id(op.dsem)) if op.is_dma else op.eng


class Prog:
    ENGINES = ("pe", "act", "dve", "pool", "sp")

    def __init__(self, nc, same_engine_sync=True):
        self.nc = nc
        self.ops = {e: [] for e in self.ENGINES}
        self.same_engine_sync = same_engine_sync
        self.nsem = 0
        self.stack = ExitStack()

    def new_sem(self, name):
        self.nsem += 1
        return self.stack.enter_context(self.nc.semaphore(f"{name}_{self.nsem}"))

    def sbuf(self, name, shape, dtype):
        return self.stack.enter_context(self.nc.sbuf_tensor(name, list(shape), dtype))

    def psum(self, name, shape, dtype):
        return self.stack.enter_context(self.nc.psum_tensor(name, list(shape), dtype))

    def _dep(self, op, prod):
        if prod is None or prod is op:
            return
        if prod.eng == op.eng and not prod.is_dma:
            if op.eng == "pe":
                return
            if not self.same_engine_sync:
                return
        if prod.is_dma:
            op.deps.append((prod, prod.dsem.count))
        else:
            prod.signal = True
            op.deps.append((prod, None))

    def _record(self, op, reads, writes):
        for b in reads:
            self._dep(op, b.last_write)
            if b.excl:
                for r in b.reads.values():
                    if r.eng != op.eng:
                        self._dep(op, r)
        for b in writes:
            self._dep(op, b.last_write)
            for r in b.reads.values():
                self._dep(op, r)
        for b in reads:
            b.reads[rkey(op)] = op
        for b in writes:
            b.last_write = op
            b.reads = {}
        self.ops[op.eng].append(op)
        return op

    def op(self, eng, fn, reads=(), writes=()):
        return self._record(Op(eng, fn), reads, writes)

    def dma(self, eng, fn, reads=(), writes=(), sem_buf=None, inc_amt=16):
        o = Op(eng, fn)
        o.is_dma = True
        o.inc_amt = inc_amt
        sb = sem_buf if sem_buf is not None else writes[0]
        if sb.dsem is None or sb.dsem.count + inc_amt > DSEM_LIMIT:
            sb.dsem = DSem(self.new_sem("d"))
        o.dsem = sb.dsem
        self._record(o, reads, writes)
        o.dsem.count += inc_amt
        return o

    def barrier(self, eng, deps):
        o = Op(eng, None)
        for p in deps:
            self._dep(o, p)
        self.ops[eng].append(o)
        return o

    def emit(self):
        nc = self.nc
        for e in self.ENGINES:
            sem = None
            cnt = 0
            for o in self.ops[e]:
                if o.is_dma or not o.signal:
                    continue
                if sem is None or cnt >= SEM_LIMIT:
                    sem = self.new_sem("e")
                    cnt = 0
                cnt += 1
                o.tok = (sem, cnt)
        engmap = {"pe": "tensor", "act": "scalar", "dve": "vector", "pool": "gpsimd", "sp": "sync"}
        with nc.Block() as block:
            for e in self.ENGINES:
                ops = self.ops[e]
                if not ops:
                    continue

                def body(engine, ops=ops):
                    waited = {}
                    for o in ops:
                        for (p, dval) in o.deps:
                            if p.is_dma:
                                s, v = p.dsem.handle, dval
                            else:
                                s, v = p.tok
                            k = s.num
                            if waited.get(k, 0) >= v:
                                continue
                            waited[k] = v
                            engine.wait_ge(s, v)
                        if o.fn is None:
                            continue
                        ins = o.fn(engine)
                        if o.is_dma:
                            ins.then_inc(o.dsem.handle, o.inc_amt)
                        elif o.signal:
                            ins.then_inc(o.tok[0], 1)

                getattr(block, engmap[e])(body)
        self.stack.close()


def retire(bufs):
    ops = {}
    for b in bufs:
        for o in ([b.last_write] if b.last_write is not None else []) + list(b.reads.values()):
            k = rkey(o)
            if k not in ops or ops[k].seq < o.seq:
                ops[k] = o
    return list(ops.values())


def inherit(bufs, ops):
    for b in bufs:
        for o in ops:
            k = rkey(o)
            if k not in b.reads or b.reads[k].seq < o.seq:
                b.reads[k] = o


def build_program(depth=DEPTH, stop=None):
    nc = bass.Bass("TRN2", target_bir_lowering=False)
    P = Prog(nc)

    def din(name, shape, dt):
        return nc.dram_tensor(name, list(shape), dt, kind="ExternalInput").ap()

    x_d = din("x", [TOK, D], F32)
    pos_d = din("pos", [128, NTT], I32)
    maskb_d = din("maskb", [128, NTB], F32)
    hmask_d = din("hmask", [128, NTB], F32)
    mixn_d = din("mix_norm", [DEPTH, D], F32)
    win_d = din("w_in", [depth, D, 8 * D], F32)
    lq1_d = din("lambda_q1", [DEPTH, 64], F32)
    lk1_d = din("lambda_k1", [DEPTH, 64], F32)
    lq2_d = din("lambda_q2", [DEPTH, 64], F32)
    lk2_d = din("lambda_k2", [DEPTH, 64], F32)
    subw_d = din("subln_w", [DEPTH, 128], F32)
    convw_d = din("conv_w", [DEPTH, 3, D], F32)
    wa_d = din("w_branch_a", [depth, D, D], F32)
    wb_d = din("w_branch_b", [depth, D, D], F32)
    wo_d = din("w_out", [depth, D, D], F32)
    ffnn_d = din("ffn_norm", [DEPTH, D], F32)
    wg_d = din("w_gate", [depth, D, DFF], F32)
    wu_d = din("w_up", [depth, D, DFF], F32)
    wd_d = din("w_down", [depth, DFF, D], F32)
    fin_d = din("final_norm", [D], F32)
    out_d = nc.dram_tensor("out", [TOK, D], F32, kind="ExternalOutput").ap()

    kvm = [nc.dram_tensor(f"kvm{i}", [128, 4096], BF16, kind="Internal").ap() for i in range(2)]
    kvg = [nc.dram_tensor(f"kvg{i}", [256, 4096], BF16, kind="Internal").ap() for i in range(2)]
    halm = nc.dram_tensor("halm", [512, 16], BF16, kind="Internal").ap()
    halg = nc.dram_tensor("halg", [1024, 16], BF16, kind="Internal").ap()
    on_dram = nc.dram_tensor("on_dram", [D, TOK], BF16, kind="Internal").ap()
    gb_dram = nc.dram_tensor("gb_dram", [D, TOK], BF16, kind="Internal").ap()
    Bkvm = [Buf("kvm0"), Buf("kvm1")]
    Bkvg = [Buf("kvg0"), Buf("kvg1")]
    Bhalm, Bhalg = Buf("halm"), Buf("halg")
    Bon_d = [Buf(f"on_d{h}") for h in range(NH)]
    Bgb_d = [Buf(f"gb_d{n}") for n in range(8)]

    xT = P.sbuf("xT", [128, 8, TOK], F32)
    BxT = [[Buf(f"xT{c}_{tb}") for tb in range(NTB)] for c in range(8)]
    xnT = P.sbuf("xnT", [128, 8, TOK], BF16)
    BxnT = [Buf(f"xnT{tb}") for tb in range(NTB)]
    BIG = P.sbuf("BIG", [128, 32768], BF16)
    ZO, KO = 0, 16384
    NSLOT = 3
    wslot = [P.sbuf(f"wslot{i}", [128, 3072], BF16) for i in range(NSLOT)]
    Bslot = [Buf(f"wslot{i}") for i in range(NSLOT)]
    NFT = 8
    ftall = P.sbuf("ftall", [128, NFT * 512], F32)
    ftmp = [ftall[:, i * 512:(i + 1) * 512] for i in range(NFT)]
    Bftmp = [Buf(f"ftmp{i}") for i in range(NFT)]
    ft_ctr = [0]

    def ft():
        i = ft_ctr[0] % NFT
        ft_ctr[0] += 1
        return ftmp[i], Bftmp[i]

    def ft2():
        if ft_ctr[0] % 2:
            ft_ctr[0] += 1
        i = ft_ctr[0] % NFT
        ft_ctr[0] += 2
        return ftall[:, i * 512:(i + 2) * 512].rearrange("p (m q) -> p m q", m=2), [Bftmp[i], Bftmp[i + 1]]

    ident_b = P.sbuf("ident_b", [128, 128], BF16)
    ident_f = P.sbuf("ident_f", [128, 128], F32)
    ones_b = P.sbuf("ones_b", [128, 128], BF16)
    ones_f = P.sbuf("ones_f", [128, 128], F32)
    Bconst = Buf("const")
    ropeC = P.sbuf("ropeC", [128, NTT, 16], F32)
    ropeS = P.sbuf("ropeS", [128, NTT, 16], F32)
    Brope = Buf("rope")
    nw = P.sbuf("nw", [128, 2 * DEPTH + 1, 8], F32)
    cw = P.sbuf("cw", [128, DEPTH, 3, 8], F32)
    neglam = P.sbuf("neglam", [128, DEPTH], F32)
    subs = P.sbuf("subs", [128, DEPTH], F32)
    maskb = P.sbuf("maskb_s", [128, NTB], F32)
    hmask = P.sbuf("hmask_s", [128, NTB], F32)
    Bparams = Buf("params")
    zh = P.sbuf("zh", [128, NTB, 16], BF16)
    Bzh = Buf("zh")
    rt1 = [P.sbuf(f"rt1_{i}", [128, 4, 16], F32) for i in range(2)]
    rt2 = [P.sbuf(f"rt2_{i}", [128, 4, 16], F32) for i in range(2)]
    Brt = [Buf(f"rt{i}") for i in range(2)]
    zero_b = P.sbuf("zero_b", [128, 16], BF16)

    psall = P.psum("psall", [128, 4096], F32)
    banks = [psall[:, i * 512:(i + 1) * 512] for i in range(8)]
    Bbank = [Buf(f"bank{i}", excl=True) for i in range(8)]
    bank_ctr = [0]
    bank_mod = [8]

    def nb():
        i = bank_ctr[0] % bank_mod[0]
        bank_ctr[0] += 1
        return banks[i], Bbank[i]

    def bigv(off, n):
        return BIG[:, off:off + n]

    zT = bigv(ZO, 16384).rearrange("p (c t) -> p c t", c=8)
    BzT = [[Buf(f"zT{c}_{tb}") for tb in range(NTB)] for c in range(8)]
    convtmp = bigv(KO, 4096).bitcast(F32)
    Bconv = Buf("convtmp")
    zx = bigv(KO + 4096, 4 * 514).rearrange("p (j t) -> p j t", j=4)
    Bzx = Buf("zx")
    mkv = [bigv(KO + 4096 * i, 4096) for i in range(2)]
    okv = [bigv(KO + 8192 + 4096 * i, 4096) for i in range(2)]
    Bmkv = [Buf("mkv0"), Buf("mkv1")]
    Bokv = [Buf("okv0"), Buf("okv1")]
    qT = [bigv(ZO + 2048 * i, 2048) for i in range(2)]
    BqT = [Buf("qT0"), Buf("qT1")]
    NPB = 8
    p1 = [bigv(ZO + 4096 + 1024 * i, 512) for i in range(NPB)]
    p2 = [bigv(ZO + 4096 + 1024 * i + 512, 512) for i in range(NPB)]
    pp = [bigv(ZO + 4096 + 1024 * i, 1024).rearrange("p (m q) -> p m q", m=2) for i in range(NPB)]
    osb = [bigv(ZO + 12288 + 2048 * i, 2048).bitcast(F32).rearrange("p (m q) -> p m q", m=2) for i in range(1)]
    Bosb = [Buf("osb0")]
    Bp1 = [Buf(f"p1_{i}") for i in range(NPB)]
    Bp2 = [Buf(f"p2_{i}") for i in range(NPB)]
    onst = [bigv(ZO + 14336 + 512 * i, 512) for i in range(2)]
    Bonst = [Buf("onst0"), Buf("onst1")]
    qkbf = [bigv(ZO + 15360 + 256 * i, 256) for i in range(2)]
    Bqkbf = [Buf("qkbf0"), Buf("qkbf1")]
    gbst = [bigv(ZO + 10240 + 512 * i, 512) for i in range(3)]
    mixT = bigv(KO, 16384).rearrange("p (c t) -> p c t", c=8)
    BmixT = [[Buf(f"mix{c}_{tb}") for tb in range(NTB)] for c in range(8)]
    hT = bigv(0, 11 * TOK).rearrange("p (c t) -> p c t", c=11)
    BhT = [[Buf(f"hT{c}_{tb}") for tb in range(NTB)] for c in range(11)]
    xstage = [bigv(2048 * i, 2048).bitcast(F32) for i in range(2)]
    Bxstage = [Buf("xst0"), Buf("xst1")]
    gbst4 = [bigv(KO + 8192 + 512 * i, 512) for i in range(3)]
    Bgbst4 = [Buf(f"gbst4_{i}") for i in range(3)]
    gbld = [P.sbuf(f"gbld{i}", [128, 512], BF16) for i in range(2)]
    Bgbld = [Buf("gbld0"), Buf("gbld1")]

    region = {"Z": [], "K": []}

    def switch(reg, new_bufs):
        ops = retire(region[reg])
        inherit(new_bufs, ops)
        region[reg] = list(new_bufs)

    def flat(ll):
        return [b for row in ll for b in row]

    wctr = [0]
    pending = []

    def kview(w2d):
        return w2d.rearrange("(kc kp) n -> kp kc n", kp=128)

    class Piece:
        def __init__(self, segs, kc):
            self.segs = segs
            self.kc = kc
            self.ncols = sum(s.shape[2] for s in segs)
            self.slot = None

        def load(self):
            s = wctr[0] % NSLOT
            wctr[0] += 1
            self.slot = s
            view = wslot[s][:, 0:self.kc * self.ncols].rearrange("p (k n) -> p k n", k=self.kc)
            off = 0
            for sg in self.segs:
                n = sg.shape[2]
                P.dma("pool", (lambda e, sg=sg, dst=view[:, :, off:off + n]: e.dma_start(out=dst, in_=sg)),
                      writes=[Bslot[s]])
                off += n
            self.view = view
            self.buf = Bslot[s]

    steps = []

    def add_piece(segs, kc, consume):
        steps.append(("piece", Piece(segs, kc), consume))

    def add_work(fn):
        steps.append(("work", None, fn))

    dyn_cache = {}

    def dynoff(e, a, b):
        en = str(e.engine)
        if (en, "rank") not in dyn_cache:
            dyn_cache[(en, "rank")] = e.partition_id() % 2
        k = (en, a, b)
        if k not in dyn_cache:
            dyn_cache[k] = e.snap((a + b * dyn_cache[(en, "rank")]) * 128)
        return dyn_cache[k]

    def mm(out_ap, lhsT, rhs, start, stop, reads, wbuf):
        P.op("pe", lambda e: e.matmul(out_ap, lhsT=lhsT, rhs=rhs, start=start, stop=stop),
             reads=reads, writes=[wbuf])

    def init_phase():
        P.op("pool", lambda e: e.memset(ones_b[:], 1.0), writes=[Bconst])
        P.op("pool", lambda e: e.memset(ones_f[:], 1.0), writes=[Bconst])
        P.op("pool", lambda e: e.memset(zero_b[:], 0.0), writes=[Bconst])
        P.op("pool", lambda e: e.memset(ident_b[:], 0.0), writes=[Bconst])
        P.op("pool", lambda e: e.affine_select(out=ident_b[:], in_=ident_b[:], compare_op=ALU.not_equal, fill=1.0,
                                               base=0, pattern=[[-1, 128]], channel_multiplier=1),
             reads=[Bconst], writes=[Bconst])
        P.op("pool", lambda e: e.memset(ident_f[:], 0.0), writes=[Bconst])
        P.op("pool", lambda e: e.affine_select(out=ident_f[:], in_=ident_f[:], compare_op=ALU.not_equal, fill=1.0,
                                               base=0, pattern=[[-1, 128]], channel_multiplier=1),
             reads=[Bconst], writes=[Bconst])
        P.dma("sp", lambda e: e.dma_start(out=maskb[:], in_=maskb_d), writes=[Bparams])
        P.dma("sp", lambda e: e.dma_start(out=hmask[:], in_=hmask_d), writes=[Bparams])
        NCD = dict(allow_slow_non_contiguous=True)
        for l in range(DEPTH):
            P.dma("sp", lambda e, l=l: e.dma_start(out=nw[:, l, :], in_=mixn_d[l].rearrange("(c p) -> p c", p=128), **NCD),
                  writes=[Bparams])
            P.dma("sp", lambda e, l=l: e.dma_start(out=nw[:, DEPTH + l, :], in_=ffnn_d[l].rearrange("(c p) -> p c", p=128), **NCD),
                  writes=[Bparams])
            for k in range(3):
                P.dma("sp", lambda e, l=l, k=k: e.dma_start(out=cw[:, l, k, :],
                                                            in_=convw_d[l, k].rearrange("(c p) -> p c", p=128), **NCD),
                      writes=[Bparams])
        P.dma("sp", lambda e: e.dma_start(out=nw[:, 2 * DEPTH, :], in_=fin_d.rearrange("(c p) -> p c", p=128), **NCD),
              writes=[Bparams])
        P.dma("sp", lambda e: e.dma_start(out=subs[:], in_=subw_d.rearrange("l d -> d l"), **NCD), writes=[Bparams])
        lt = [P.sbuf(f"lam{i}", [128, DEPTH, 64], F32) for i in range(4)]
        Blam = Buf("lam")
        for i, d_ in enumerate([lq1_d, lk1_d, lq2_d, lk2_d]):
            P.dma("sp", lambda e, i=i, d_=d_: e.dma_start(out=lt[i][:], in_=d_.partition_broadcast(128)), writes=[Blam])
        ls = P.sbuf("lsum", [128, 2, DEPTH], F32)
        P.op("dve", lambda e: e.tensor_tensor(out=lt[0][:], in0=lt[0][:], in1=lt[1][:], op=ALU.mult), reads=[Blam], writes=[Blam])
        P.op("dve", lambda e: e.tensor_tensor(out=lt[2][:], in0=lt[2][:], in1=lt[3][:], op=ALU.mult), reads=[Blam], writes=[Blam])
        P.op("dve", lambda e: e.reduce_sum(out=ls[:, 0, :], in_=lt[0][:], axis=AX.X), reads=[Blam], writes=[Blam])
        P.op("dve", lambda e: e.reduce_sum(out=ls[:, 1, :], in_=lt[2][:], axis=AX.X), reads=[Blam], writes=[Blam])
        P.op("act", lambda e: e.activation(out=ls[:], in_=ls[:], func=AF.Exp), reads=[Blam], writes=[Blam])
        P.op("dve", lambda e: e.tensor_tensor(out=neglam[:], in0=ls[:, 1, :], in1=ls[:, 0, :], op=ALU.subtract),
             reads=[Blam], writes=[Bparams])
        for l in range(DEPTH):
            li = 0.8 - 0.6 * math.exp(-0.3 * l)
            P.op("dve", lambda e, l=l, li=li: e.tensor_scalar(out=neglam[:, l:l + 1], in0=neglam[:, l:l + 1], scalar1=-li,
                                                              scalar2=None, op0=ALU.add), reads=[Bparams], writes=[Bparams])
            P.op("dve", lambda e, l=l, li=li: e.tensor_scalar(out=subs[:, l:l + 1], in0=subs[:, l:l + 1], scalar1=1.0 - li,
                                                              scalar2=None, op0=ALU.mult), reads=[Bparams], writes=[Bparams])
        posf = P.sbuf("posf", [128, NTT], F32)
        ang = P.sbuf("ang", [128, NTT, 8], F32)
        tq = P.sbuf("tq", [128, NTT, 8], F32)
        rn = P.sbuf("rn", [128, NTT, 8], F32)
        P.dma("pool", lambda e: e.dma_start(out=posf[:], in_=pos_d), writes=[Brope])
        for j in range(8):
            fr = float(np.float32(THETA) ** np.float32(-(2.0 * j) / 16.0))
            P.op("dve", lambda e, j=j, fr=fr: e.tensor_scalar(out=ang[:, :, j], in0=posf[:], scalar1=fr, scalar2=None,
                                                              op0=ALU.mult), reads=[Brope], writes=[Brope])
        P.op("dve", lambda e: e.tensor_scalar(out=ang[:], in0=ang[:], scalar1=1.0 / (2 * math.pi), scalar2=None,
                                              op0=ALU.mult), reads=[Brope], writes=[Brope])
        for which in range(2):
            if which == 1:
                P.op("dve", lambda e: e.tensor_scalar(out=ang[:], in0=ang[:], scalar1=0.25, scalar2=None, op0=ALU.add),
                     reads=[Brope], writes=[Brope])
            P.op("dve", lambda e: e.tensor_scalar(out=rn[:], in0=ang[:], scalar1=MAGIC, scalar2=None, op0=ALU.add),
                 reads=[Brope], writes=[Brope])
            P.op("dve", lambda e: e.tensor_scalar(out=rn[:], in0=rn[:], scalar1=-MAGIC, scalar2=None, op0=ALU.add),
                 reads=[Brope], writes=[Brope])
            P.op("dve", lambda e: e.tensor_tensor(out=tq[:], in0=ang[:], in1=rn[:], op=ALU.subtract),
                 reads=[Brope], writes=[Brope])
            if which == 0:
                P.op("act", lambda e: e.activation(out=ropeS[:, :, 8:16], in_=tq[:], func=AF.Sin, scale=2 * math.pi),
                     reads=[Brope], writes=[Brope])
                P.op("act", lambda e: e.activation(out=ropeS[:, :, 0:8], in_=tq[:], func=AF.Sin, scale=-2 * math.pi),
                     reads=[Brope], writes=[Brope])
            else:
                P.op("act", lambda e: e.activation(out=ropeC[:, :, 0:8], in_=tq[:], func=AF.Sin, scale=2 * math.pi),
                     reads=[Brope], writes=[Brope])
                P.op("act", lambda e: e.activation(out=ropeC[:, :, 8:16], in_=tq[:], func=AF.Sin, scale=2 * math.pi),
                     reads=[Brope], writes=[Brope])
        switch("Z", Bxstage)
        for tt in range(NTT):
            s = tt % 2
            P.dma("sp", lambda e, tt=tt, s=s: e.dma_start(out=xstage[s], in_=x_d[tt * 128:(tt + 1) * 128, :]),
                  writes=[Bxstage[s]])
            tb = tt // 4
            for c4 in range(2):
                bk, Bbk = nb()
                for i in range(4):
                    c = c4 * 4 + i
                    P.op("pe", lambda e, bk=bk, i=i, c=c, s=s: e.transpose(out=bk[:, i * 128:(i + 1) * 128],
                                                                        in_=xstage[s][:, c * 128:(c + 1) * 128],
                                                                        identity=ident_f[:]),
                         reads=[Bxstage[s], Bconst], writes=[Bbk])
                dst = xT[:, c4 * 4:(c4 + 1) * 4, tt * 128:(tt + 1) * 128]
                src = bk[:].rearrange("p (c t) -> p c t", c=4)
                eng = "dve" if c4 == 0 else "act"
                if eng == "dve":
                    P.op("dve", lambda e, dst=dst, src=src: e.tensor_copy(out=dst, in_=src), reads=[Bbk],
                         writes=[BxT[c4 * 4 + i][tb] for i in range(4)])
                else:
                    P.op("act", lambda e, dst=dst, src=src: e.copy(out=dst, in_=src), reads=[Bbk],
                         writes=[BxT[c4 * 4 + i][tb] for i in range(4)])

    def norm_phase(widx):
        for tb in range(NTB):
            ts = slice(tb * 512, (tb + 1) * 512)
            bk, Bbk = nb()
            for c in range(8):
                sq, Bsq = ft()
                sqb = sq.bitcast(BF16)[:, 0:512]
                P.op("dve" if c % 4 != 3 else "pool", lambda e, sqb=sqb, c=c, ts=ts: e.tensor_tensor(
                    out=sqb, in0=xT[:, c, ts], in1=xT[:, c, ts], op=ALU.mult),
                     reads=[BxT[c][tb]], writes=[Bsq])
                mm(bk[:], ones_b[:], sqb, c == 0, c == 7, [Bsq, Bconst], Bbk)
            rs, Brs = ft()
            P.op("act", lambda e, rs=rs, bk=bk: e.activation(out=rs[:], in_=bk[:], func=AF.Ln, bias=EPS, scale=1.0 / D),
                 reads=[Bbk], writes=[Brs])
            P.op("act", lambda e, rs=rs: e.activation(out=rs[:], in_=rs[:], func=AF.Exp, scale=-0.5),
                 reads=[Brs], writes=[Brs])
            for c in range(8):
                P.op("dve", lambda e, c=c, ts=ts, rs=rs: e.scalar_tensor_tensor(
                    out=xnT[:, c, ts], in0=xT[:, c, ts], scalar=nw[:, widx, c:c + 1], in1=rs[:],
                    op0=ALU.mult, op1=ALU.mult),
                     reads=[BxT[c][tb], Brs, Bparams], writes=[BxnT[tb]])

    def proj(bk, Bbk, pv, pbuf, col0, rhs_fn, rhs_bufs, nk):
        for kc in range(nk):
            mm(bk[:], pv[:, kc, col0:col0 + 128], rhs_fn(kc), kc == 0, kc == nk - 1, [pbuf] + rhs_bufs, Bbk)

    def branch_b(l):
        W = kview(win_d[l])

        def d1(c):
            def consume(pc):
                for tb in range(NTB):
                    ts = slice(tb * 512, (tb + 1) * 512)
                    b0, Bb0 = nb()
                    b1, Bb1 = nb()
                    proj(b0, Bb0, pc.view, pc.buf, 0, lambda kc: xnT[:, kc, ts], [BxnT[tb]], 8)
                    proj(b1, Bb1, pc.view, pc.buf, 128, lambda kc: xnT[:, kc, ts], [BxnT[tb]], 8)
                    t, Bt = ft()
                    P.op("act", lambda e, t=t, b0=b0: e.copy(out=t[:], in_=b0[:]), reads=[Bb0], writes=[Bt])
                    P.op("dve", lambda e, t=t, b1=b1, c=c, ts=ts: e.tensor_tensor(out=zT[:, c, ts], in0=b1[:], in1=t[:],
                                                                                  op=ALU.mult),
                         reads=[Bb1, Bt], writes=[BzT[c][tb]])
            return consume

        for c in range(8):
            add_piece([W[:, :, 4096 + c * 128:4096 + (c + 1) * 128], W[:, :, 5120 + c * 128:5120 + (c + 1) * 128]], 8, d1(c))

        def halo():
            for j in range(NTB):
                P.dma("sp", lambda e, j=j: e.dma_start(
                    out=halm[j * 128:(j + 1) * 128, :].rearrange("p (c t) -> p c t", c=8),
                    in_=zT[:, :, j * 512 + 510:j * 512 + 512]),
                      reads=[BzT[c][j] for c in range(8)], writes=[Bhalm])
            P.dma("pool", lambda e: e.collective_compute("AllGather", ALU.bypass, replica_groups=GROUPS,
                                                         ins=[halm[:, :]], outs=[halg[:, :]]),
                  reads=[Bhalm], writes=[Bhalg], inc_amt=1)
            coef = [(0, 0), (5, -1), (1, 1), (7, -1)]
            for j in range(NTB):
                a, b = coef[j]
                P.dma("sp", lambda e, j=j, a=a, b=b: e.dma_start(
                    out=zh[:, j, :],
                    in_=halg[bass.ds(dynoff(e, a, b), 128), :]),
                      reads=[Bhalg], writes=[Bzh])
            P.op("pool", lambda e: e.tensor_tensor(out=zh[:], in0=zh[:], in1=hmask[:, :].unsqueeze(2).to_broadcast([128, NTB, 16]),
                                                   op=ALU.mult), reads=[Bzh, Bparams], writes=[Bzh])
        add_work(halo)

        def d3(c):
            def consume(pc):
                P.op("pool", lambda e: e.tensor_copy(out=zx[:, :, 2:514],
                                                     in_=zT[:, c, :].rearrange("p (j t) -> p j t", j=4)),
                     reads=BzT[c], writes=[Bzx])
                P.op("pool", lambda e: e.tensor_copy(out=zx[:, :, 0:2],
                                                     in_=zh[:, :, 2 * c:2 * c + 2]),
                     reads=[Bzh], writes=[Bzx])
                cv = convtmp.rearrange("p (j t) -> p j t", j=4)
                P.op("dve", lambda e: e.tensor_scalar(out=cv, in0=zx[:, :, 2:514], scalar1=cw[:, l, 2, c:c + 1],
                                                      scalar2=None, op0=ALU.mult),
                     reads=[Bzx, Bparams], writes=[Bconv])
                P.op("dve", lambda e: e.scalar_tensor_tensor(out=cv, in0=zx[:, :, 1:513], scalar=cw[:, l, 1, c:c + 1],
                                                             in1=cv, op0=ALU.mult, op1=ALU.add),
                     reads=[Bzx, Bparams, Bconv], writes=[Bconv])
                P.op("dve", lambda e: e.scalar_tensor_tensor(out=cv, in0=zx[:, :, 0:512], scalar=cw[:, l, 0, c:c + 1],
                                                             in1=cv, op0=ALU.mult, op1=ALU.add),
                     reads=[Bzx, Bparams, Bconv], writes=[Bconv])
                for tb in range(NTB):
                    ts = slice(tb * 512, (tb + 1) * 512)
                    b0, Bb0 = nb()
                    proj(b0, Bb0, pc.view, pc.buf, 0, lambda kc: xnT[:, kc, ts], [BxnT[tb]], 8)
                    P.op("dve", lambda e, b0=b0, ts=ts: e.tensor_tensor(out=zT[:, c, ts], in0=b0[:], in1=convtmp[:, ts],
                                                                        op=ALU.mult),
                         reads=[Bb0, Bconv], writes=[BzT[c][tb]])
            return consume

        for c in range(8):
            add_piece([W[:, :, 3072 + c * 128:3072 + (c + 1) * 128]], 8, d3(c))

        Wb = kview(wb_d[l])

        def d4(n):
            def consume(pc):
                for tb in range(NTB):
                    ts = slice(tb * 512, (tb + 1) * 512)
                    b0, Bb0 = nb()
                    b1, Bb1 = nb()
                    proj(b0, Bb0, pc.view, pc.buf, 0, lambda kc: zT[:, kc, ts], [BzT[kc][tb] for kc in range(8)], 8)
                    proj(b1, Bb1, pc.view, pc.buf, 128, lambda kc: xnT[:, kc, ts], [BxnT[tb]], 8)
                    t, Bt = ft()
                    P.op("act", lambda e, t=t, b1=b1: e.activation(out=t[:], in_=b1[:], func=AF.Sigmoid),
                         reads=[Bb1], writes=[Bt])
                    si = (n * NTB + tb) % 3
                    P.op("dve", lambda e, t=t, b0=b0, si=si: e.tensor_tensor(out=gbst4[si], in0=b0[:], in1=t[:], op=ALU.mult),
                         reads=[Bb0, Bt], writes=[Bgbst4[si]])
                    P.dma("sp", lambda e, si=si, n=n, ts=ts: e.dma_start(out=gb_dram[n * 128:(n + 1) * 128, ts], in_=gbst4[si]),
                          reads=[Bgbst4[si]], writes=[Bgb_d[n]])
            return consume

        for n in range(8):
            add_piece([Wb[:, :, n * 128:(n + 1) * 128], W[:, :, 7168 + n * 128:7168 + (n + 1) * 128]], 8, d4(n))

    def qkv_head(l, h):
        W = kview(win_d[l])
        s = h % 2

        def consume(pc):
            tpb = [None]

            def do_mm_rope(tt):
                bk, Bbk = nb()
                for kc in range(8):
                    mm(bk[:, 0:384], xnT[:, kc, tt * 128:(tt + 1) * 128], pc.view[:, kc, :], kc == 0, kc == 7,
                       [pc.buf, BxnT[tt // 4]], Bbk)
                qs = tt % 2
                qk, Bqk = qkbf[qs], Bqkbf[qs]
                r1, r2, Br = rt1[qs], rt2[qs], Brt[qs]
                g4 = bk[:, 0:256].rearrange("p (g d) -> p g d", g=4)
                P.op("dve", lambda e, qk=qk, bk=bk: e.tensor_copy(out=qk, in_=bk[:, 0:256]), reads=[Bbk], writes=[Bqk])
                P.op("dve", lambda e, r1=r1, g4=g4, tt=tt: e.tensor_tensor(
                    out=r1[:], in0=g4[:, :, 0:16], in1=ropeC[:, tt:tt + 1, :].to_broadcast([128, 4, 16]), op=ALU.mult),
                     reads=[Bbk, Brope], writes=[Br])
                P.op("dve", lambda e, r2=r2, g4=g4, tt=tt: e.tensor_tensor(
                    out=r2[:, :, 0:8], in0=g4[:, :, 8:16], in1=ropeS[:, tt:tt + 1, 0:8].to_broadcast([128, 4, 8]), op=ALU.mult),
                     reads=[Bbk, Brope], writes=[Br])
                P.op("dve", lambda e, r2=r2, g4=g4, tt=tt: e.tensor_tensor(
                    out=r2[:, :, 8:16], in0=g4[:, :, 0:8], in1=ropeS[:, tt:tt + 1, 8:16].to_broadcast([128, 4, 8]), op=ALU.mult),
                     reads=[Bbk, Brope], writes=[Br])
                P.op("dve", lambda e, bk=bk, tt=tt: e.tensor_copy(out=mkv[s][:, 2048 + tt * 128:2048 + (tt + 1) * 128],
                                                                  in_=bk[:, 256:384]),
                     reads=[Bbk], writes=[Bmkv[s]])
                P.op("dve", lambda e, qk=qk, r1=r1, r2=r2: e.tensor_tensor(
                    out=qk.rearrange("p (g d) -> p g d", g=4)[:, :, 0:16], in0=r1[:], in1=r2[:], op=ALU.add),
                     reads=[Br], writes=[Bqk])

            def do_tr(tt):
                qs = tt % 2
                qk, Bqk = qkbf[qs], Bqkbf[qs]
                if tt % 4 == 0:
                    tpb[0] = nb()
                tb_, Btb = tpb[0]
                tv = tb_[:].bitcast(BF16)
                i4 = tt % 4
                P.op("pe", lambda e, tv=tv, qk=qk, i4=i4: e.transpose(out=tv[:, i4 * 128:(i4 + 1) * 128], in_=qk[:, 0:128],
                                                                      identity=ident_b[:]),
                     reads=[Bqk, Bconst], writes=[Btb])
                P.op("pe", lambda e, tv=tv, qk=qk, i4=i4: e.transpose(out=tv[:, 512 + i4 * 128:512 + (i4 + 1) * 128],
                                                                      in_=qk[:, 128:256], identity=ident_b[:]),
                     reads=[Bqk, Bconst], writes=[Btb])
                if tt % 4 == 3:
                    t0 = (tt // 4) * 512
                    P.op("dve", lambda e, tv=tv, t0=t0: e.tensor_copy(out=qT[s][:, t0:t0 + 512], in_=tv[:, 0:512]),
                         reads=[Btb], writes=[BqT[s]])
                    P.op("dve", lambda e, tv=tv, t0=t0: e.tensor_copy(out=mkv[s][:, t0:t0 + 512], in_=tv[:, 512:1024]),
                         reads=[Btb], writes=[Bmkv[s]])

            for tt in range(NTT):
                do_mm_rope(tt)
                if tt > 0:
                    do_tr(tt - 1)
            do_tr(NTT - 1)
            P.dma("sp", lambda e: e.dma_start(out=kvm[s][:, :], in_=mkv[s]), reads=[Bmkv[s]], writes=[Bkvm[s]])
            P.dma("pool", lambda e: e.collective_compute("AllGather", ALU.bypass, replica_groups=GROUPS,
                                                         ins=[kvm[s][:, :]], outs=[kvg[s][:, :]]),
                  reads=[Bkvm[s]], writes=[Bkvg[s]], inc_amt=1)
            P.dma("pool", lambda e: e.dma_start(out=okv[s], in_=kvg[s][bass.ds(dynoff(e, 1, -1), 128), :]),
                  reads=[Bkvg[s]], writes=[Bokv[s]])

        add_piece([W[:, :, h * 128:(h + 1) * 128], W[:, :, 1024 + h * 128:1024 + (h + 1) * 128],
                   W[:, :, 2048 + h * 128:2048 + (h + 1) * 128]], 8, consume)

    pend = {"A": None, "B": None, "S": None}
    sctr = [0]

    def misc_banks(pair):
        if pair is None:
            return [nb(), nb()]
        return [(banks[2 * pair], Bbank[2 * pair]), (banks[2 * pair + 1], Bbank[2 * pair + 1])]

    def flush_pending(pair=None):
        if pend["S"] is not None:
            f = pend["S"]
            pend["S"] = None
            f()
        if pend["A"] is not None:
            f = pend["A"]
            pend["A"] = None
            f(pair)
        if pend["B"] is not None:
            f = pend["B"]
            pend["B"] = None
            f(pair)

    def attention_head(l, h):
        s = h % 2
        pctr = [0]
        O1, BO1 = banks[4], Bbank[4]
        O2, BO2 = banks[5], Bbank[5]
        S1, BS1 = banks[6], Bbank[6]
        S2, BS2 = banks[7], Bbank[7]
        O12 = psall[:, 4 * 512:6 * 512].rearrange("p (m q) -> p m q", m=2)
        S12 = psall[:, 6 * 512:8 * 512].rearrange("p (m q) -> p m q", m=2)
        for j in range(NTB):
            tiles = []
            for lb in range(j):
                for r in range(4):
                    tiles.append(("m", 4 * lb + r, 0, None))
                for r in range(4):
                    tiles.append(("o", 4 * lb + r, 0, None))
            for r in range(4):
                tiles.append(("o", 4 * j + r, 0, "maybe"))
            for r in range(4):
                tiles.append(("m", 4 * j + r, 128 * r, "diag"))
            nt = len(tiles)
            nq = nt // 4
            q0 = j * 512

            def scores(i):
                src, kt, c0, kind = tiles[i]
                kvb, Bkvb = (mkv[s], Bmkv[s]) if src == "m" else (okv[s], Bokv[s])
                pair = sctr[0] % 2
                sctr[0] += 1
                sa, Bsa = banks[2 * pair], Bbank[2 * pair]
                sb_, Bsb = banks[2 * pair + 1], Bbank[2 * pair + 1]
                mm(sa[:, c0:512], kvb[0:64, kt * 128:(kt + 1) * 128], qT[s][0:64, q0 + c0:q0 + 512], True, True,
                   [Bkvb, BqT[s]], Bsa)
                mm(sb_[:, c0:512], kvb[64:128, kt * 128:(kt + 1) * 128], qT[s][64:128, q0 + c0:q0 + 512], True, True,
                   [Bkvb, BqT[s]], Bsb)
                return (sa, Bsa, sb_, Bsb, pair)

            def padd(dst, src, c0=0):
                P.op("dve", lambda e: e.tensor_tensor(out=pp[dst][:, :, c0:512], in0=pp[dst][:, :, c0:512],
                                                      in1=pp[src][:, :, c0:512], op=ALU.add),
                     reads=[Bp1[dst], Bp1[src], Bp2[dst], Bp2[src]], writes=[Bp1[dst], Bp2[dst]])

            def make_sums(nq_):
                q_, n_ = [], [0]

                def push(at, yb):
                    q_.append((at, yb))

                def emit(upto):
                    while q_ and q_[0][0] <= upto:
                        _, yb = q_.pop(0)
                        st_, sp2_ = (n_[0] == 0), (n_[0] == nq_ - 1)
                        n_[0] += 1
                        mm(S1[:], ones_b[:], pp[yb][:, 0, :], st_, sp2_, [Bconst, Bp1[yb]], BS1)
                        mm(S2[:], ones_b[:], pp[yb][:, 1, :], st_, sp2_, [Bconst, Bp2[yb]], BS2)
                return push, emit

            push_sum, emit_sums = make_sums(nq)

            scq = [scores(0), scores(1)]
            if pend["S"] is not None:
                f = pend["S"]
                pend["S"] = None
                f()
            quad = []
            for i in range(nt):
                src, kt, c0, kind = tiles[i]
                kvb, Bkvb = (mkv[s], Bmkv[s]) if src == "m" else (okv[s], Bokv[s])
                sa, Bsa, sb_, Bsb, cur_pair = scq.pop(0)
                y = pctr[0] % NPB
                pctr[0] += 1
                bias = maskb[:, j:j + 1] if kind == "maybe" else 0.0
                rb = [Bparams] if kind == "maybe" else []
                s2 = psall[:, (2 * cur_pair) * 512:(2 * cur_pair + 2) * 512].rearrange("p (m q) -> p m q", m=2)
                P.op("act", lambda e, y=y, s2=s2, c0=c0, bias=bias: e.activation(out=pp[y][:, :, c0:512], in_=s2[:, :, c0:512],
                                                                                func=AF.Exp, bias=bias, scale=0.125),
                     reads=[Bsa, Bsb] + rb, writes=[Bp1[y], Bp2[y]])
                if kind == "diag":
                    P.op("dve", lambda e, y=y, c0=c0: e.memset(pp[y][64:128, :, c0:c0 + 64], 0.0), reads=[],
                         writes=[Bp1[y], Bp2[y]])
                if i == 2 and pend["A"] is not None:
                    f = pend["A"]
                    pend["A"] = None
                    f(cur_pair)
                if i == 7 and pend["B"] is not None:
                    f = pend["B"]
                    pend["B"] = None
                    f(cur_pair)
                if i + 2 < nt:
                    scq.append(scores(i + 2))
                vt = kvb[:, 2048 + kt * 128:2048 + (kt + 1) * 128]
                st, sp_ = (i == 0), (i == nt - 1)
                mm(O1[:, c0:512], vt, p1[y][:, c0:512], st, sp_, [Bkvb, Bp1[y]], BO1)
                mm(O2[:, c0:512], vt, p2[y][:, c0:512], st, sp_, [Bkvb, Bp2[y]], BO2)
                ph = i % 4
                if ph == 0:
                    quad = [(y, c0)]
                else:
                    quad.append((y, c0))
                if ph == 1:
                    padd(quad[0][0], quad[1][0], quad[1][1])
                if ph == 3:
                    padd(quad[2][0], quad[3][0], quad[3][1])
                    padd(quad[0][0], quad[2][0], quad[2][1])
                    push_sum(i + 2, quad[0][0])
                emit_sums(i)
            flush_pending()
            pend["S"] = (lambda es=emit_sums: es(10 ** 9))
            rr, Brr = ft2()
            ob, Bob = osb[0], Bosb[0]
            P.op("dve", lambda e, ob=ob: e.tensor_copy(out=ob, in_=O12), reads=[BO1, BO2], writes=[Bob])

            def stage_a(pair, rr=rr, Brr=Brr, ob=ob, Bob=Bob, j=j, q0=q0):
                P.op("act", lambda e: e.activation(out=rr, in_=S12, func=AF.Ln), reads=[BS1, BS2, Bob], writes=Brr)
                P.op("act", lambda e: e.activation(out=rr, in_=rr, func=AF.Exp, scale=-1.0), reads=Brr, writes=Brr)
                P.op("dve", lambda e: e.tensor_tensor(out=rr, in0=ob, in1=rr, op=ALU.mult),
                     reads=[Bob] + Brr, writes=Brr)
                o_, Bo = ft()
                P.op("dve", lambda e: e.scalar_tensor_tensor(out=o_[:], in0=rr[:, 1, :], scalar=neglam[:, l:l + 1],
                                                             in1=rr[:, 0, :], op0=ALU.mult, op1=ALU.add),
                     reads=Brr + [Bparams], writes=[Bo])
                sq, Bsq = ft()
                sqb = sq.bitcast(BF16)[:, 0:512]
                P.op("dve", lambda e: e.tensor_tensor(out=sqb, in0=o_[:], in1=o_[:], op=ALU.mult),
                     reads=[Bo], writes=[Bsq])

                def stage_b(pair):
                    bq, Bbq = misc_banks(pair)[0]
                    mm(bq[:], ones_b[:], sqb, True, True, [Bsq, Bconst], Bbq)
                    rs, Brs = ft()
                    P.op("act", lambda e: e.activation(out=rs[:], in_=bq[:], func=AF.Ln, bias=EPS, scale=1.0 / 128),
                         reads=[Bbq], writes=[Brs])
                    P.op("act", lambda e: e.activation(out=rs[:], in_=rs[:], func=AF.Exp, scale=-0.5),
                         reads=[Brs], writes=[Brs])
                    so = (h * NTB + j) % 2
                    P.op("dve", lambda e: e.scalar_tensor_tensor(out=onst[so], in0=o_[:], scalar=subs[:, l:l + 1],
                                                                 in1=rs[:], op0=ALU.mult, op1=ALU.mult),
                         reads=[Bo, Brs, Bparams], writes=[Bonst[so]])
                    P.dma("sp", lambda e: e.dma_start(out=on_dram[h * 128:(h + 1) * 128, q0:q0 + 512], in_=onst[so]),
                          reads=[Bonst[so]], writes=[Bon_d[h]])
                pend["B"] = stage_b

            pend["A"] = stage_a

    def mix_out(l):
        W = kview(win_d[l])
        Wa = kview(wa_d[l])
        Wo = kview(wo_d[l])

        def load_on():
            switch("Z", flat(BzT))
            switch("K", flat(BmixT))
            for h in range(NH):
                P.dma("sp", lambda e, h=h: e.dma_start(out=zT[:, h, :], in_=on_dram[h * 128:(h + 1) * 128, :]),
                      reads=[Bon_d[h]], writes=BzT[h])
        add_work(load_on)

        def d5(n):
            def consume(pc):
                for tb in range(NTB):
                    ts = slice(tb * 512, (tb + 1) * 512)
                    gi = (n * NTB + tb) % 2
                    P.dma("sp", lambda e, gi=gi, ts=ts: e.dma_start(out=gbld[gi][:], in_=gb_dram[n * 128:(n + 1) * 128, ts]),
                          reads=[Bgb_d[n]], writes=[Bgbld[gi]])
                    b0, Bb0 = nb()
                    b1, Bb1 = nb()
                    proj(b0, Bb0, pc.view, pc.buf, 0, lambda kc: zT[:, kc, ts], [BzT[kc][tb] for kc in range(8)], 8)
                    proj(b1, Bb1, pc.view, pc.buf, 128, lambda kc: xnT[:, kc, ts], [BxnT[tb]], 8)
                    t, Bt = ft()
                    P.op("act", lambda e, t=t, b1=b1: e.activation(out=t[:], in_=b1[:], func=AF.Sigmoid),
                         reads=[Bb1], writes=[Bt])
                    P.op("dve", lambda e, t=t, b0=b0: e.tensor_tensor(out=t[:], in0=b0[:], in1=t[:], op=ALU.mult),
                         reads=[Bb0, Bt], writes=[Bt])
                    P.op("pool", lambda e, t=t, gi=gi, ts=ts: e.tensor_tensor(out=mixT[:, n, ts], in0=t[:], in1=gbld[gi][:],
                                                                              op=ALU.add),
                         reads=[Bt, Bgbld[gi]], writes=[BmixT[n][tb]])
            return consume

        for n in range(8):
            add_piece([Wa[:, :, n * 128:(n + 1) * 128], W[:, :, 6144 + n * 128:6144 + (n + 1) * 128]], 8, d5(n))

        def d6(n):
            def consume(pc):
                for tb in range(NTB):
                    ts = slice(tb * 512, (tb + 1) * 512)
                    b0, Bb0 = nb()
                    proj(b0, Bb0, pc.view, pc.buf, 0, lambda kc: mixT[:, kc, ts], [BmixT[kc][tb] for kc in range(8)], 8)
                    P.op("dve", lambda e, b0=b0, ts=ts: e.tensor_tensor(out=xT[:, n, ts], in0=xT[:, n, ts], in1=b0[:],
                                                                        op=ALU.add),
                         reads=[Bb0, BxT[n][tb]], writes=[BxT[n][tb]])
            return consume

        for n in range(8):
            add_piece([Wo[:, :, n * 128:(n + 1) * 128]], 8, d6(n))

    def ffn(l):
        Wg = kview(wg_d[l])
        Wu = kview(wu_d[l])
        Wd = wd_d[l].rearrange("(fc fp) n -> fp fc n", fp=128)
        add_work(lambda: norm_phase(DEPTH + l))
        for half in range(2):
            def sw():
                switch("Z", flat(BhT))
                switch("K", [])
            if half == 0:
                def sw0():
                    ops = retire(region["Z"]) + retire(region["K"])
                    inherit(flat(BhT), ops)
                    region["Z"] = flat(BhT)
                    region["K"] = []
                add_work(sw0)

            def gu(fc, fl):
                def consume(pc):
                    for tb in range(NTB):
                        ts = slice(tb * 512, (tb + 1) * 512)
                        b0, Bb0 = nb()
                        b1, Bb1 = nb()
                        proj(b0, Bb0, pc.view, pc.buf, 0, lambda kc: xnT[:, kc, ts], [BxnT[tb]], 8)
                        proj(b1, Bb1, pc.view, pc.buf, 128, lambda kc: xnT[:, kc, ts], [BxnT[tb]], 8)
                        t, Bt = ft()
                        P.op("act", lambda e, t=t, b0=b0: e.activation(out=t[:], in_=b0[:], func=AF.Silu),
                             reads=[Bb0], writes=[Bt])
                        P.op("dve", lambda e, t=t, b1=b1, ts=ts: e.tensor_tensor(out=hT[:, fl, ts], in0=b1[:], in1=t[:],
                                                                                 op=ALU.mult),
                             reads=[Bb1, Bt], writes=[BhT[fl][tb]])
                return consume

            for fl in range(11):
                fc = half * 11 + fl
                add_piece([Wg[:, :, fc * 128:(fc + 1) * 128], Wu[:, :, fc * 128:(fc + 1) * 128]], 8, gu(fc, fl))

            def dn(n):
                def consume(pc):
                    for tb in range(NTB):
                        ts = slice(tb * 512, (tb + 1) * 512)
                        b0, Bb0 = nb()
                        proj(b0, Bb0, pc.view, pc.buf, 0, lambda kc: hT[:, kc, ts], [BhT[kc][tb] for kc in range(11)], 11)
                        P.op("dve", lambda e, b0=b0, ts=ts: e.tensor_tensor(out=xT[:, n, ts], in0=xT[:, n, ts], in1=b0[:],
                                                                            op=ALU.add),
                             reads=[Bb0, BxT[n][tb]], writes=[BxT[n][tb]])
                return consume

            for n in range(8):
                add_piece([Wd[:, half * 11:(half + 1) * 11, n * 128:(n + 1) * 128]], 11, dn(n))

    def final_phase():
        ops = retire(region["Z"]) + retire(region["K"])
        inherit(Bxstage, ops)
        region["Z"] = list(Bxstage)
        region["K"] = []
        outs = []
        for tb in range(NTB):
            ts = slice(tb * 512, (tb + 1) * 512)
            bk, Bbk = nb()
            for c in range(8):
                sq, Bsq = ft()
                sqb = sq.bitcast(BF16)[:, 0:512]
                P.op("dve" if c % 4 != 3 else "pool", lambda e, sqb=sqb, c=c, ts=ts: e.tensor_tensor(
                    out=sqb, in0=xT[:, c, ts], in1=xT[:, c, ts], op=ALU.mult),
                     reads=[BxT[c][tb]], writes=[Bsq])
                mm(bk[:], ones_b[:], sqb, c == 0, c == 7, [Bsq, Bconst], Bbk)
            rs, Brs = ft()
            P.op("act", lambda e, rs=rs, bk=bk: e.activation(out=rs[:], in_=bk[:], func=AF.Ln, bias=EPS, scale=1.0 / D),
                 reads=[Bbk], writes=[Brs])
            P.op("act", lambda e, rs=rs: e.activation(out=rs[:], in_=rs[:], func=AF.Exp, scale=-0.5),
                 reads=[Brs], writes=[Brs])
            for c in range(8):
                P.op("dve", lambda e, c=c, ts=ts, rs=rs: e.scalar_tensor_tensor(
                    out=xT[:, c, ts], in0=xT[:, c, ts], scalar=nw[:, 2 * DEPTH, c:c + 1], in1=rs[:],
                    op0=ALU.mult, op1=ALU.mult),
                     reads=[BxT[c][tb], Brs, Bparams], writes=[BxT[c][tb]])
            for t4 in range(4):
                tt = tb * 4 + t4
                s = tt % 2
                for c4 in range(2):
                    bt, Bbt = nb()
                    for i in range(4):
                        c = c4 * 4 + i
                        P.op("pe", lambda e, bt=bt, i=i, c=c, tt=tt: e.transpose(out=bt[:, i * 128:(i + 1) * 128],
                                                                              in_=xT[:, c, tt * 128:(tt + 1) * 128],
                                                                              identity=ident_f[:]),
                             reads=[BxT[c][tb], Bconst], writes=[Bbt])
                    if c4 == 0:
                        P.op("dve", lambda e, bt=bt, s=s: e.tensor_copy(out=xstage[s][:, 0:512], in_=bt[:]),
                             reads=[Bbt], writes=[Bxstage[s]])
                    else:
                        P.op("act", lambda e, bt=bt, s=s: e.copy(out=xstage[s][:, 512:1024], in_=bt[:]),
                             reads=[Bbt], writes=[Bxstage[s]])
                outs.append(P.dma("sp", lambda e, tt=tt, s=s: e.dma_start(out=out_d[tt * 128:(tt + 1) * 128, :], in_=xstage[s]),
                                  reads=[Bxstage[s]], writes=[Buf(f"out{tt}")], sem_buf=Bxstage[s]))
        P.barrier("sp", outs)

    add_work(init_phase)
    for l in range(depth):
        if stop is not None and stop < 1:
            break
        add_work(lambda l=l: norm_phase(l))
        if stop is not None and stop < 2:
            break

        def sw_b():
            ops = retire(region["Z"]) + retire(region["K"])
            inherit(flat(BzT), ops)
            inherit([Bconv, Bzx] + Bgbst4, ops)
            region["Z"] = flat(BzT)
            region["K"] = [Bconv, Bzx] + Bgbst4
        add_work(sw_b)
        branch_b(l)
        if stop is not None and stop < 2.3:
            break

        def sw_att():
            switch("Z", BqT + Bp1 + Bp2 + Bonst + Bqkbf + Bosb)
            switch("K", Bmkv + Bokv)
            bank_mod[0] = 4
        add_work(sw_att)
        qkv_head(l, 0)
        if stop is not None and stop < 2.4:
            break
        qkv_head(l, 1)
        if stop is not None and stop < 2.6:
            break
        for h in range(NH):
            add_work(lambda l=l, h=h: attention_head(l, h))
            if stop is not None and stop < 2.8:
                break
            if h + 2 < NH:
                qkv_head(l, h + 2)
            if stop is not None and stop < 3 and h + 1 >= round((stop - 2.8) * 100):
                break

        def sw_dense():
            flush_pending()
            bank_mod[0] = 8
        add_work(sw_dense)
        if stop is not None and stop < 4:
            break
        mix_out(l)
        if stop is not None and stop < 5:
            break
        ffn(l)
    add_work(final_phase)

    piece_idx = [i for i, st in enumerate(steps) if st[0] == "piece"]
    loaded = [0]

    def load_upto(k):
        while loaded[0] < len(piece_idx) and loaded[0] < k:
            steps[piece_idx[loaded[0]]][1].load()
            loaded[0] += 1

    np_done = 0
    load_upto(NSLOT)
    for st in steps:
        if st[0] == "work":
            st[2]()
        else:
            st[2](st[1])
            np_done += 1
            load_upto(np_done + NSLOT)
    P.emit()
    return nc


def _blocks(p):
    return [2 * j + (p ^ (j & 1)) for j in range(4)]


_NC_CACHE = {}


def kernel(x, positions, mix_norm, w_in, lambda_q1, lambda_k1, lambda_q2, lambda_k2,
           subln_w, conv_w, w_branch_a, w_branch_b, w_out, ffn_norm,
           w_gate, w_up, w_down, final_norm, _depth=DEPTH, _stop=None):
    x = np.asarray(x, dtype=np.float32)
    positions = np.asarray(positions, dtype=np.int32)
    shared = {
        "mix_norm": mix_norm, "w_in": w_in, "lambda_q1": lambda_q1, "lambda_k1": lambda_k1,
        "lambda_q2": lambda_q2, "lambda_k2": lambda_k2, "subln_w": subln_w, "conv_w": conv_w,
        "w_branch_a": w_branch_a, "w_branch_b": w_branch_b, "w_out": w_out, "ffn_norm": ffn_norm,
        "w_gate": w_gate, "w_up": w_up, "w_down": w_down, "final_norm": final_norm,
    }
    shared = {k: np.ascontiguousarray(np.asarray(v, dtype=np.float32)) for k, v in shared.items()}
    if _depth < DEPTH:
        for k in ("w_in", "w_branch_a", "w_branch_b", "w_out", "w_gate", "w_up", "w_down"):
            shared[k] = np.ascontiguousarray(shared[k][:_depth])
    in_maps = []
    for c in range(8):
        b, p = c // 2, c % 2
        idx = np.concatenate([np.arange(512 * g, 512 * (g + 1)) for g in _blocks(p)])
        xs = np.ascontiguousarray(x[b, idx, :])
        ps = np.ascontiguousarray(positions[b, idx].reshape(NTT, 128).T)
        mb = np.zeros((128, NTB), np.float32)
        for j in range(NTB):
            if (p ^ (j & 1)) != 1:
                mb[:, j] = NEG
        hm = np.ones((128, NTB), np.float32)
        if p == 0:
            hm[:, 0] = 0.0
        m = {"x": xs, "pos": ps, "maskb": mb, "hmask": hm}
        m.update(shared)
        in_maps.append(m)
    if (_depth, _stop) not in _NC_CACHE:
        _NC_CACHE[(_depth, _stop)] = build_program(_depth, _stop)
    nc = _NC_CACHE[(_depth, _stop)]
    res = run_bass_kernel_spmd(nc, in_maps, core_ids=list(range(8)))
    out = np.empty((4, 4096, D), np.float32)
    for c in range(8):
        b, p = c // 2, c % 2
        idx = np.concatenate([np.arange(512 * g, 512 * (g + 1)) for g in _blocks(p)])
        out[b, idx, :] = np.asarray(res.results[c]["out"], dtype=np.float32)
    return out
```
